# Optimizing a Trainium2 kernel written in Bass

```python
import jax, jax.numpy as jnp
from jax import lax
import numpy as np

D_MODEL = 1024
BATCH = 16
SEQ = 256
DEPTH = 2
DEC_BATCH = 2
DEC_SEQ = 4096
PAST_LEN = 512

GRID_W = 64
HEAD_DIM = 64
N_HEADS_A = 16
N_KV_A = 4
WINDOW = 128
BLOCK = 128
N_HEADS_B = 16
WIN_R = 8
WIN_C = 16
D_FF = 2816
CONV_W = 3
ROPE_BASE = 10000.0
EPS = 1e-6
SCALE = HEAD_DIM ** -0.5
N_A = (DEPTH + 1) // 2
N_B = DEPTH // 2

kernel_name = 'hybrid_prefix_diffusion_step'


def rmsnorm(x, w):
    xf = x.astype(jnp.float32)
    y = xf * lax.rsqrt(jnp.mean(xf * xf, axis=-1, keepdims=True) + EPS)
    return (y * w.astype(jnp.float32)).astype(x.dtype)


def adaln(cond, w_ada, b_ada):
    m = jax.nn.silu(cond) @ w_ada + b_ada
    return [t[:, None, :] for t in jnp.split(m, 6, axis=-1)]


def modulate(h, shift, scale):
    return h * (1 + scale) + shift


def project_qkv(h, w_qkv, qn, kn, n_heads, n_kv):
    b, l, _ = h.shape
    qkv = h @ w_qkv
    q, k, v = jnp.split(qkv, [n_heads * HEAD_DIM, (n_heads + n_kv) * HEAD_DIM], axis=-1)
    q = rmsnorm(q.reshape(b, l, n_heads, HEAD_DIM), qn)
    k = rmsnorm(k.reshape(b, l, n_kv, HEAD_DIM), kn)
    v = v.reshape(b, l, n_kv, HEAD_DIM)
    return q, k, v


def rope_1d(x, pos):
    half = x.shape[-1] // 2
    freqs = ROPE_BASE ** (-jnp.arange(half, dtype=jnp.float32) / half)
    ang = pos.astype(jnp.float32)[:, None] * freqs[None, :]
    cos = jnp.cos(ang)[:, None, :]
    sin = jnp.sin(ang)[:, None, :]
    xf = x.astype(jnp.float32)
    x1, x2 = xf[..., :half], xf[..., half:]
    return jnp.concatenate([x1 * cos - x2 * sin, x2 * cos + x1 * sin], axis=-1).astype(x.dtype)


def rope_2d(x):
    t = jnp.arange(x.shape[1])
    half = HEAD_DIM // 2
    return jnp.concatenate([rope_1d(x[..., :half], t // GRID_W), rope_1d(x[..., half:], t % GRID_W)], axis=-1)


def context_attention(q, k, v, sink):
    b, lc, h, _ = q.shape
    n_kv = k.shape[2]
    g = h // n_kv
    qg = q.reshape(b, lc, n_kv, g, HEAD_DIM)
    s = jnp.einsum('bqkgd,bckd->bkgqc', qg, k).astype(jnp.float32) * SCALE
    if sink is not None:
        sk = jnp.broadcast_to(sink.astype(jnp.float32).reshape(n_kv, g)[None, :, :, None, None], s.shape[:-1] + (1,))
        s = jnp.concatenate([s, sk], axis=-1)
    p = jax.nn.softmax(s, axis=-1)[..., :lc].astype(v.dtype)
    o = jnp.einsum('bkgqc,bckd->bqkgd', p, v)
    return o.reshape(b, lc, h * HEAD_DIM)


def window_attention_latent(q, k, v, kc, vc, sink):
    b, l, h, _ = q.shape
    n_kv = k.shape[2]
    g = h // n_kv
    lc = kc.shape[1]
    nb = l // BLOCK
    qb = q.reshape(b, nb, BLOCK, n_kv, g, HEAD_DIM)
    pad = ((0, 0), (BLOCK, BLOCK), (0, 0), (0, 0))
    kp = jnp.pad(k, pad).reshape(b, nb + 2, BLOCK, n_kv, HEAD_DIM)
    vp = jnp.pad(v, pad).reshape(b, nb + 2, BLOCK, n_kv, HEAD_DIM)
    kb = jnp.concatenate([kp[:, :-2], kp[:, 1:-1], kp[:, 2:]], axis=2)
    vb = jnp.concatenate([vp[:, :-2], vp[:, 1:-1], vp[:, 2:]], axis=2)
    qpos = jnp.arange(nb)[:, None] * BLOCK + jnp.arange(BLOCK)[None, :]
    kpos = jnp.arange(nb)[:, None] * BLOCK - BLOCK + jnp.arange(3 * BLOCK)[None, :]
    valid = (jnp.abs(qpos[:, :, None] - kpos[:, None, :]) <= WINDOW) & (kpos[:, None, :] >= 0) & (kpos[:, None, :] < l)
    s_loc = jnp.einsum('bnqkgd,bnjkd->bkgnqj', qb, kb).astype(jnp.float32) * SCALE
    s_loc = jnp.where(valid[None, None, None], s_loc, -jnp.inf)
    s_ctx = jnp.einsum('bnqkgd,bckd->bkgnqc', qb, kc).astype(jnp.float32) * SCALE
    s_sink = jnp.broadcast_to(sink.astype(jnp.float32).reshape(n_kv, g)[None, :, :, None, None, None], s_loc.shape[:-1] + (1,))
    p = jax.nn.softmax(jnp.concatenate([s_loc, s_ctx, s_sink], axis=-1), axis=-1)
    p_loc = p[..., :3 * BLOCK].astype(v.dtype)
    p_ctx = p[..., 3 * BLOCK:3 * BLOCK + lc].astype(v.dtype)
    o = jnp.einsum('bkgnqj,bnjkd->bnqkgd', p_loc, vb) + jnp.einsum('bkgnqc,bckd->bnqkgd', p_ctx, vc)
    return o.reshape(b, l, h * HEAD_DIM)


def neighborhood_attention_latent(q, k, v, kc, vc, rpb):
    b, l, h, _ = q.shape
    rows = l // GRID_W
    wr = min(WIN_R, rows)
    r = jnp.arange(rows)
    rs = jnp.clip(r - wr // 2, 0, rows - wr)
    key_rows = rs[:, None] + jnp.arange(wr)[None, :]
    qg = q.reshape(b, rows, GRID_W, h, HEAD_DIM)
    kg = k.reshape(b, rows, GRID_W, h, HEAD_DIM)[:, key_rows]
    vg = v.reshape(b, rows, GRID_W, h, HEAD_DIM)[:, key_rows]
    col = jnp.arange(GRID_W)
    cs = jnp.clip(col - WIN_C // 2, 0, GRID_W - WIN_C)
    col_ok = (col[None, :] >= cs[:, None]) & (col[None, :] < cs[:, None] + WIN_C)
    dr = key_rows - r[:, None]
    dc = jnp.clip(col[None, :] - col[:, None], -(WIN_C - 1), WIN_C - 1)
    bias = rpb.astype(jnp.float32)[:, dr[:, None, :, None] + WIN_R - 1, dc[None, :, None, :] + WIN_C - 1]
    s_loc = jnp.einsum('brqhd,brikhd->bhrqik', qg, kg).astype(jnp.float32) * SCALE + bias[None]
    s_loc = jnp.where(col_ok[None, None, None, :, None, :], s_loc, -jnp.inf)
    s_loc = s_loc.reshape(b, h, rows, GRID_W, wr * GRID_W)
    s_ctx = jnp.einsum('brqhd,bchd->bhrqc', qg, kc).astype(jnp.float32) * SCALE
    p = jax.nn.softmax(jnp.concatenate([s_loc, s_ctx], axis=-1), axis=-1)
    p_loc = p[..., :wr * GRID_W].astype(v.dtype)
    p_ctx = p[..., wr * GRID_W:].astype(v.dtype)
    o = jnp.einsum('bhrqj,brjhd->brqhd', p_loc, vg.reshape(b, rows, wr * GRID_W, h, HEAD_DIM))
    o = o + jnp.einsum('bhrqc,bchd->brqhd', p_ctx, vc)
    return o.reshape(b, l, h * HEAD_DIM)


def conv_ffn(h, w_up, conv_w, conv_b, w_down):
    l = h.shape[1]
    u = h @ w_up
    half = CONV_W // 2
    up = jnp.pad(u, ((0, 0), (half, half), (0, 0)))
    u = sum(up[:, o:o + l] * conv_w[o] for o in range(CONV_W)) + conv_b
    gate, val = jnp.split(u, 2, axis=-1)
    return (jax.nn.silu(gate) * val) @ w_down


def setup_inputs(seed: int = 0) -> dict:
    key = jax.random.key(seed)
    ks = jax.random.split(key, 26)
    f32 = jnp.float32

    def nrm(k, shape, scale):
        return jax.random.normal(k, shape, f32) * scale

    qkv_a = (N_HEADS_A + 2 * N_KV_A) * HEAD_DIM
    qkv_b = 3 * N_HEADS_B * HEAD_DIM
    return {
        'x_prompt': nrm(ks[0], (BATCH, SEQ, D_MODEL), 1.0),
        'x_sample': nrm(ks[1], (DEC_BATCH, DEC_SEQ, D_MODEL), 1.0),
        'cache_k_a': nrm(ks[2], (DEC_BATCH, N_A, PAST_LEN, N_KV_A, HEAD_DIM), 1.0),
        'cache_v_a': nrm(ks[3], (DEC_BATCH, N_A, PAST_LEN, N_KV_A, HEAD_DIM), 1.0),
        'cache_k_b': nrm(ks[4], (DEC_BATCH, N_B, PAST_LEN, N_HEADS_B, HEAD_DIM), 1.0),
        'cache_v_b': nrm(ks[5], (DEC_BATCH, N_B, PAST_LEN, N_HEADS_B, HEAD_DIM), 1.0),
        'c': nrm(ks[6], (DEC_BATCH, D_MODEL), 1.0),
        'c_ctx': nrm(ks[7], (D_MODEL,), 1.0),
        'norm_attn_w': 1.0 + nrm(ks[8], (DEPTH, D_MODEL), 0.05),
        'norm_ffn_w': 1.0 + nrm(ks[9], (DEPTH, D_MODEL), 0.05),
        'w_ada': nrm(ks[10], (DEPTH, D_MODEL, 6 * D_MODEL), D_MODEL ** -0.5),
        'b_ada': nrm(ks[11], (DEPTH, 6 * D_MODEL), 0.02),
        'w_qkv_a': nrm(ks[12], (N_A, D_MODEL, qkv_a), D_MODEL ** -0.5),
        'q_norm_a': 1.0 + nrm(ks[13], (N_A, HEAD_DIM), 0.05),
        'k_norm_a': 1.0 + nrm(ks[14], (N_A, HEAD_DIM), 0.05),
        'sink_a': nrm(ks[15], (N_A, N_HEADS_A), 0.5),
        'w_o_a': nrm(ks[16], (N_A, N_HEADS_A * HEAD_DIM, D_MODEL), (N_HEADS_A * HEAD_DIM) ** -0.5),
        'w_qkv_b': nrm(ks[17], (N_B, D_MODEL, qkv_b), D_MODEL ** -0.5),
        'q_norm_b': 1.0 + nrm(ks[18], (N_B, HEAD_DIM), 0.05),
        'k_norm_b': 1.0 + nrm(ks[19], (N_B, HEAD_DIM), 0.05),
        'rpb_b': nrm(ks[20], (N_B, N_HEADS_B, 2 * WIN_R - 1, 2 * WIN_C - 1), 0.5),
        'w_o_b': nrm(ks[21], (N_B, N_HEADS_B * HEAD_DIM, D_MODEL), (N_HEADS_B * HEAD_DIM) ** -0.5),
        'w_up': nrm(ks[22], (DEPTH, D_MODEL, 2 * D_FF), D_MODEL ** -0.5),
        'conv_w': nrm(ks[23], (DEPTH, CONV_W, 2 * D_FF), 0.5),
        'conv_b': nrm(ks[24], (DEPTH, 2 * D_FF), 0.02),
        'w_down': nrm(ks[25], (DEPTH, D_FF, D_MODEL), D_FF ** -0.5),
    }


def reference(x_prompt, x_sample, cache_k_a, cache_v_a, cache_k_b, cache_v_b, c, c_ctx,
              norm_attn_w, norm_ffn_w, w_ada, b_ada,
              w_qkv_a, q_norm_a, k_norm_a, sink_a, w_o_a,
              w_qkv_b, q_norm_b, k_norm_b, rpb_b, w_o_b,
              w_up, conv_w, conv_b, w_down):
    xp = x_prompt
    xs = x_sample
    new_k_a, new_v_a, new_k_b, new_v_b = [], [], [], []
    for i in range(DEPTH):
        j = i // 2
        mp = adaln(c_ctx[None, :], w_ada[i], b_ada[i])
        ms = adaln(c, w_ada[i], b_ada[i])
        hp = modulate(rmsnorm(xp, norm_attn_w[i]), mp[0], mp[1])
        hs = modulate(rmsnorm(xs, norm_attn_w[i]), ms[0], ms[1])
        if i % 2 == 0:
            q, k, v = project_qkv(hp, w_qkv_a[j], q_norm_a[j], k_norm_a[j], N_HEADS_A, N_KV_A)
            op = context_attention(q, k, v, sink_a[j]) @ w_o_a[j]
            new_k_a.append(k)
            new_v_a.append(v)
            q, k, v = project_qkv(hs, w_qkv_a[j], q_norm_a[j], k_norm_a[j], N_HEADS_A, N_KV_A)
            o_s = window_attention_latent(rope_2d(q), rope_2d(k), v, cache_k_a[:, j], cache_v_a[:, j], sink_a[j]) @ w_o_a[j]
        else:
            q, k, v = project_qkv(hp, w_qkv_b[j], q_norm_b[j], k_norm_b[j], N_HEADS_B, N_HEADS_B)
            op = context_attention(q, k, v, None) @ w_o_b[j]
            new_k_b.append(k)
            new_v_b.append(v)
            q, k, v = project_qkv(hs, w_qkv_b[j], q_norm_b[j], k_norm_b[j], N_HEADS_B, N_HEADS_B)
            o_s = neighborhood_attention_latent(q, k, v, cache_k_b[:, j], cache_v_b[:, j], rpb_b[j]) @ w_o_b[j]
        xp = xp + mp[2] * op
        xs = xs + ms[2] * o_s
        hp = modulate(rmsnorm(xp, norm_ffn_w[i]), mp[3], mp[4])
        hs = modulate(rmsnorm(xs, norm_ffn_w[i]), ms[3], ms[4])
        xp = xp + mp[5] * conv_ffn(hp, w_up[i], conv_w[i], conv_b[i], w_down[i])
        xs = xs + ms[5] * conv_ffn(hs, w_up[i], conv_w[i], conv_b[i], w_down[i])
    return (xp, xs, jnp.stack(new_k_a, axis=1), jnp.stack(new_v_a, axis=1), jnp.stack(new_k_b, axis=1), jnp.stack(new_v_b, axis=1))
```

```python
from contextlib import ExitStack
import numpy as np
import concourse.bass as bass
import concourse.mybir as mybir
from concourse.bass_utils import run_bass_kernel_spmd

F32 = mybir.dt.float32
BF16 = mybir.dt.bfloat16
AF = mybir.ActivationFunctionType
ALU = mybir.AluOpType
AX = mybir.AxisListType

ENGS = ("pe", "act", "dve", "pool", "sp")
NDMA = {"sp": 24, "pool": 6}
NROT = 4
PSUM_KEYS = ("pq0", "pq1", "tpb", "S0", "S1", "Ops", "y0", "y1")
SCHED = True
FUSE_WAIT = True
VERBOSE = False
PRIO_MODE = 1
PRIO_Q = 2000.0
INTERLEAVE = lambda g: False
WAITDBG = None


class Prog:
    def __init__(self, nc):
        self.nc = nc
        self.ops = []
        self.stack = ExitStack()
        self.n_sb = 0

    def sb(self, shape, dt, name=None):
        self.n_sb += 1
        return self.stack.enter_context(self.nc.sbuf_tensor(name or f"sb{self.n_sb}", list(shape), dt))

    def ps(self, shape, dt, name=None):
        self.n_sb += 1
        return self.stack.enter_context(self.nc.psum_tensor(name or f"ps{self.n_sb}", list(shape), dt))

    def mark(self, name):
        if not hasattr(self, "marks"):
            self.marks = []
        self.marks.append((name, len(self.ops)))

    def barrier(self, scratch):
        self.ops.append(["pool", lambda e: e.memset(scratch, 0.0), (), ("__bar__",), False, True, 100.0])

    def add(self, eng, fn, r=(), w=(), dma=False, cost=None):
        pr = [k for k in r if k in PSUM_KEYS]
        if pr:
            r = [k for k in r if k not in PSUM_KEYS]
            w = list(w) + pr
        if cost is None:
            cost = {"pe": 100.0, "act": 400.0, "dve": 300.0, "pool": 500.0, "sp": 2500.0}[eng]
        self.ops.append([eng, fn, tuple(r) + ("__bar__",), tuple(w), dma, False, float(cost)])

    def dma(self, out, in_, r=(), w=(), eng="sp", **kw):
        try:
            nbytes = out.free_size() * 4 * 128
        except Exception:
            nbytes = 65536
        self.add(eng, lambda e: e.dma_start(out=out, in_=in_, **kw), r, w, dma=True, cost=2200.0 + nbytes / 150.0)

    def schedule(self, deps):
        import heapq
        ops = self.ops
        n = len(ops)
        succ = [[] for _ in range(n)]
        indeg = [0] * n
        for i in range(n):
            indeg[i] = len(deps[i])
            for d in deps[i]:
                succ[d].append(i)
        prio = list(range(n))
        if PRIO_MODE == 1:
            bl = [0.0] * n
            for i in range(n - 1, -1, -1):
                m = 0.0
                for j in succ[i]:
                    if bl[j] > m:
                        m = bl[j]
                bl[i] = m + ops[i][6]
            prio = [(-int(bl[i] / PRIO_Q), i) for i in range(n)]
        ready = {e: [] for e in ENGS}
        for i in range(n):
            if indeg[i] == 0:
                heapq.heappush(ready[ops[i][0]], (prio[i], i))
        busy = {e: 0.0 for e in ENGS}
        events = []
        order = []
        now = 0.0
        SYNC = 150.0
        self.t_start = [0.0] * n
        self.t_fin = [0.0] * n
        rel = [0.0] * n
        while len(order) < n:
            progressed = False
            for e in ENGS:
                if busy[e] <= now and ready[e]:
                    _p, i = heapq.heappop(ready[e])
                    c = ops[i][6]
                    if ops[i][4]:
                        issue = 1000.0 if e == "pool" else 60.0
                        busy[e] = now + issue
                        fin = now + issue + c
                    else:
                        busy[e] = now + c
                        fin = now + c
                    heapq.heappush(events, (fin, i))
                    order.append(i)
                    self.t_start[i] = now
                    self.t_fin[i] = fin
                    progressed = True
            if progressed:
                continue
            cand = [busy[e] for e in ENGS if busy[e] > now and ready[e]]
            tn = min(cand) if cand else None
            if events and (tn is None or events[0][0] <= tn):
                t, i = heapq.heappop(events)
                now = max(now, t)
                if i < 0:
                    j = -i - 1
                    heapq.heappush(ready[ops[j][0]], (prio[j], j))
                    continue
                ei = ops[i][0]
                for j in succ[i]:
                    ej = ops[j][0]
                    lat = 0.0 if (ei == "pe" and ej == "pe") else (70.0 if ei == ej else SYNC)
                    rel[j] = max(rel[j], t + lat)
                    indeg[j] -= 1
                    if indeg[j] == 0:
                        if rel[j] <= now:
                            heapq.heappush(ready[ej], (prio[j], j))
                        else:
                            heapq.heappush(events, (rel[j], -j - 1))
            else:
                assert tn is not None, "scheduler stuck"
                now = tn
        self.model_span = now
        if WAITDBG:
            t0, t1 = WAITDBG
            eng_ops = {e: [] for e in ENGS}
            for i in order:
                eng_ops[ops[i][0]].append(i)
            for e in ENGS:
                prev_end = 0.0
                wait_by = {}
                busy_t = 0.0
                for i in eng_ops[e]:
                    st_ = self.t_start[i]
                    if t0 <= st_ < t1:
                        gap = st_ - prev_end
                        if gap > 1.0 and deps[i]:
                            crit = max(deps[i], key=lambda d: self.t_fin[d])
                            kk = ops[crit][0] + ("-dma" if ops[crit][4] else "")
                            wait_by[kk] = wait_by.get(kk, 0.0) + gap
                        busy_t += (self.t_fin[i] - st_) if not ops[i][4] else 0.0
                    prev_end = max(prev_end, self.t_fin[i] if not ops[i][4] else st_ + 60.0)
                print(f"   [{e}] busy {busy_t/1e3:7.1f}us of {(t1-t0)/1e3:.1f}; waits on:", {k: round(v / 1e3, 1) for k, v in wait_by.items()})
        if VERBOSE and getattr(self, "marks", None):
            mk_ = self.marks + [("end", n)]
            for (nm, a), (_, b_) in zip(mk_[:-1], mk_[1:]):
                if b_ > a:
                    pe_b = sum(ops[i][6] for i in range(a, b_) if ops[i][0] == "pe")
                    ac_b = sum(ops[i][6] for i in range(a, b_) if ops[i][0] == "act")
                    dv_b = sum(ops[i][6] for i in range(a, b_) if ops[i][0] == "dve")
                    po_b = sum(ops[i][6] for i in range(a, b_) if ops[i][0] == "pool" and not ops[i][4])
                    print(f"  phase {nm:12s} ops {b_-a:6d} t=[{min(self.t_start[a:b_])/1e3:8.1f},{max(self.t_fin[a:b_])/1e3:8.1f}]us pe={pe_b/1e3:7.1f} act={ac_b/1e3:7.1f} dve={dv_b/1e3:7.1f} pool={po_b/1e3:7.1f}")
        bs = {e: 0.0 for e in ENGS}
        for o in ops:
            bs[o[0]] += (60.0 if o[4] and o[0] == "sp" else 1000.0 if o[4] else o[6])
        self.model_busy = bs
        return order

    def emit(self):
        nc = self.nc
        ops = self.ops
        n = len(ops)
        last_w = {}
        readers = {}
        deps = [None] * n
        last_bar = -1
        for i, (eng, fn, r, w, isd, isbar, _c) in enumerate(ops):
            d = set()
            if isbar:
                d.update(range(last_bar + 1, i))
                last_bar = i
            for k in r:
                if k in last_w:
                    d.add(last_w[k])
            for k in w:
                if k in last_w:
                    d.add(last_w[k])
                if k != "__bar__":
                    for x in readers.get(k, ()):
                        d.add(x)
            d.discard(i)
            deps[i] = d
            for k in w:
                last_w[k] = i
                readers[k] = []
            for k in r:
                if k not in w:
                    readers.setdefault(k, []).append(i)
        if SCHED:
            order = self.schedule(deps)
            pos = {o: p for p, o in enumerate(order)}
            ops = [ops[o] for o in order]
            deps = [set(pos[d] for d in deps[o]) for o in order]
            self.ops = ops
        need_sig = [False] * n
        fdeps = [None] * n
        for i in range(n):
            eng = ops[i][0]
            best = {}
            dm = []
            for d in deps[i]:
                if ops[d][4]:
                    dm.append(d)
                else:
                    e2 = ops[d][0]
                    if e2 == "pe" and eng == "pe" and not ops[i][4]:
                        continue
                    if e2 not in best or best[e2] < d:
                        best[e2] = d
            fdeps[i] = (best, dm)
        seen = {e: {} for e in ENGS}
        seen_d = {e: set() for e in ENGS}
        clk = [None] * n
        for i in range(n):
            eng = ops[i][0]
            best, dm = fdeps[i]
            nb = {}
            for e2, d in sorted(best.items(), key=lambda kv: -kv[1]):
                if seen[eng].get(e2, -1) >= d:
                    continue
                nb[e2] = d
                need_sig[d] = True
                sd = seen[eng]
                if sd.get(e2, -1) < d:
                    sd[e2] = d
                if clk[d] is not None:
                    for e3, v3 in clk[d].items():
                        if sd.get(e3, -1) < v3:
                            sd[e3] = v3
            nd = [d for d in dm if d not in seen_d[eng]]
            for d in nd:
                seen_d[eng].add(d)
            fdeps[i] = (nb, nd)
            if not ops[i][4]:
                clk[i] = dict(seen[eng])
        sems = {}
        for e in ("pe", "act", "dve", "pool"):
            sems[e] = [self.stack.enter_context(nc.semaphore(f"c_{e}{j}")) for j in range(NROT)]
        dsems = {}
        for e in ("sp", "pool"):
            dsems[e] = [self.stack.enter_context(nc.semaphore(f"d_{e}{j}")) for j in range(NDMA[e])]
        sig = [None] * n
        cnt = {e: 0 for e in ENGS}
        dcnt = {e: 0 for e in ENGS}
        prev_dma = [None] * n
        for i in range(n):
            eng, fn, r, w, isd, _b, _c = ops[i]
            if isd:
                j = dcnt[eng]
                dcnt[eng] += 1
                nd_ = NDMA[eng]
                sig[i] = (dsems[eng][j % nd_], 16 * (j // nd_ + 1), 16)
                if j >= nd_:
                    prev_dma[i] = (dsems[eng][j % nd_], 16 * (j // nd_))
            elif need_sig[i]:
                j = cnt[eng]
                cnt[eng] += 1
                sig[i] = (sems[eng][j % NROT], j // NROT + 1, 1)
        self.stats = dict(n=n, span=getattr(self, "model_span", None), busy=getattr(self, "model_busy", None), sig=dict(cnt), dma=dict(dcnt),
                          per_eng={e: sum(1 for o in ops if o[0] == e) for e in ENGS})
        per = {e: [i for i in range(n) if ops[i][0] == e] for e in ENGS}

        def run(e_name, eobj):
            for i in per[e_name]:
                eng, fn, r, w, isd, _b, _c = ops[i]
                nb, nd = fdeps[i]
                waits = [prev_dma[i]] if prev_dma[i] is not None else []
                waits += [sig[d][:2] for d in sorted(list(nb.values()) + list(nd))]
                for s_, v_ in (waits[:-1] if FUSE_WAIT else waits):
                    eobj.wait_ge(s_, v_)
                ins = fn(eobj)
                if FUSE_WAIT and waits:
                    ins._wait_ge(*waits[-1])
                if sig[i] is not None:
                    ins.then_inc(sig[i][0], sig[i][2])
            j = dcnt[e_name]
            if j > 0:
                nd_ = NDMA[e_name]
                for q in range(min(j, nd_)):
                    uses = (j - 1 - q) // nd_ + 1
                    eobj.wait_ge(dsems[e_name][q], 16 * uses)

        with nc.Block() as block:
            @block.tensor
            def _(e):
                run("pe", e)

            @block.scalar
            def _(e):
                run("act", e)

            @block.vector
            def _(e):
                run("dve", e)

            @block.gpsimd
            def _(e):
                run("pool", e)

            @block.sync
            def _(e):
                run("sp", e)
        self.stack.close()


D = 1024
SEQ = 256
DEC_SEQ = 4096
PAST = 512
HD = 64
DFF = 2816
EPS = 1e-6
SCALE = HD ** -0.5
NEG = -30000.0
NSLAB = 16
STAGE = 99
ATT_SUB = 9
FFN_SUB = 9
ATT_STREAMS = (0, 1)


class StopBuild(Exception):
    pass


def chk(k):
    if k > STAGE:
        raise StopBuild()

B_KLIST = {3: (1, 5), 4: (2, 7), 5: (3, 7), 6: (4, 8), 7: (5, 9), 8: (6, 10), 9: (7, 11),
           10: (8, 12), 11: (8, 13), 12: (10, 13)}
B_QBLKS = list(range(3, 13))
B_UNITS = {}
for _q in B_QBLKS:
    for _k in range(B_KLIST[_q][0], B_KLIST[_q][1] + 1):
        B_UNITS[(_q, _k)] = len(B_UNITS)
NUNIT = len(B_UNITS)


class Rot:
    def __init__(self, items):
        self.items = items
        self.i = 0

    def get(self):
        x = self.items[self.i % len(self.items)]
        self.i += 1
        return x


def build():
    nc = bass.Bass("TRN2", target_bir_lowering=False)

    def din(name, shape):
        return nc.dram_tensor(name, list(shape), F32, kind="ExternalInput").ap()

    def dout(name, shape):
        return nc.dram_tensor(name, list(shape), F32, kind="ExternalOutput").ap()

    xp = din("xp", [512, D]); xs = din("xs", [NSLAB * 128, D]); cond = din("cond", [16, 128])
    nw = din("nw", [32, 128]); bada = din("bada", [2, 48, 128])
    wada = din("wada", [2, D, 6 * D]); wga = din("wga", [4, D, 448]); wgb = din("wgb", [4, D, 768])
    woa = din("woa", [D, D]); wob = din("wob", [D, D]); wup = din("wup", [2, D, 2 * DFF]); wdn = din("wdn", [2, DFF, D])
    qkwa = din("qkwa", [1, 384]); qkwb = din("qkwb", [1, 512]); sink = din("sink", [1, 16])
    convw = din("convw", [2, 132, 128]); convb = din("convb", [2, 44, 128])
    cka = din("cka", [PAST, 256]); cva = din("cva", [PAST, 256]); ckb = din("ckb", [PAST, D]); cvb = din("cvb", [PAST, D])
    ident = din("ident", [128, 128]); mprev = din("mprev", [128, 128]); mnext = din("mnext", [128, 128])
    ropeC = din("ropeC", [NSLAB * 128, 64]); ropeS = din("ropeS", [NSLAB * 128, 64])
    vala = din("vala", [128, NSLAB]); flags = din("flags", [128, 2])
    btab = din("btab", [4, 128, 4 * 7 * 128]); cmask = din("cmask", [128, 128])
    invt = din("invt", [2, NUNIT * 128]); negr = din("negr", [2, 128])
    yp = dout("yp", [512, D]); ys = dout("ys", [1024, D])
    nka = dout("nka", [512, 256]); nva = dout("nva", [512, 256]); nkb = dout("nkb", [512, D]); nvb = dout("nvb", [512, D])

    P = Prog(nc)
    pq = [P.ps([128, 512], F32, f"pq{i}") for i in range(2)]
    tpbF = P.ps([128, 512], F32, "tpbF")
    tpb = tpbF[:, :].bitcast(BF16)
    S2 = P.ps([128, 1024], F32, "S2")
    Sps = [S2[:, 0:512], S2[:, 512:1024]]
    Ops = P.ps([128, 512], F32, "Ops")
    Y2 = P.ps([128, 1024], F32, "Y2")
    yps = [Y2[:, 0:512], Y2[:, 512:1024]]

    XRp = P.sb([128, 4, D], F32, "XRp")
    XRs = P.sb([128, 14, D], F32, "XRs")
    Gb = P.sb([128, 2, D], F32, "Gb")
    id_f = P.sb([128, 128], F32, "id_f"); id_b = P.sb([128, 128], BF16, "id_b")
    ones_f = P.sb([128, 128], F32, "ones_f")
    mk = P.sb([128, 2, 128], BF16, "mk")
    cT = P.sb([128, 16], F32, "cT"); scT = P.sb([128, 16], BF16, "scT")
    nwT = P.sb([128, 32], F32, "nwT"); badaT = P.sb([128, 96], F32, "badaT")
    cwT = P.sb([128, 2, 132], F32, "cwT"); cbT = P.sb([128, 2, 44], F32, "cbT")
    modTs = [P.sb([128, 48, 2], F32, f"modT{l}") for l in range(2)]
    ABs = [P.sb([128, 2, 4, 8], F32, f"AB{l}") for l in range(2)]
    qkw = P.sb([128, 768], F32, "qkw"); esink = P.sb([128, 16], F32, "esink")
    valt = P.sb([128, NSLAB], F32, "valt"); ones_c = P.sb([128, 1], F32, "ones_c"); flg = P.sb([128, 2], F32, "flg")
    negb = P.sb([2, 128], BF16, "negb")
    epsc = P.sb([128, 1], F32, "epsc")
    scr = P.sb([128, 8], F32, "scr")
    ARENA_F = 26368
    arena = P.sb([128, ARENA_F], F32, "arena")
    aoff = [0]

    def carve(nelem, dt):
        words = nelem if dt == F32 else (nelem + 1) // 2
        words = (words + 7) // 8 * 8
        a = arena[:, aoff[0]:aoff[0] + words]
        aoff[0] += words
        assert aoff[0] <= ARENA_F, aoff[0]
        return a if dt == F32 else a.bitcast(BF16)[:, 0:nelem]

    def arena_reset():
        if VERBOSE:
            print("arena used words", aoff[0], "of", ARENA_F)
        P.barrier(scr[:, 0:1])
        aoff[0] = 0

    def mm(out, lhsT, rhs, start, stop, r, w, skip=False):
        P.add("pe", lambda e: e.matmul(out, lhsT, rhs, start=start, stop=stop, skip_group_check=skip), r, w,
              cost=35.0 + max(rhs.free_size(), 64) / 2.4)

    def tr(out, in_, idm, r, w):
        P.add("pe", lambda e: e.transpose(out, in_, idm), r, w, cost=90.0)

    def fcost(eng, out):
        f = out.free_size()
        return {"act": 230.0 + f / 1.1, "dve": 90.0 + f / 0.8, "pool": 150.0 + f / 0.42}[eng]

    def act(out, in_, func, r, w, bias=None, scale=None, accum=None):
        kw = {}
        if bias is not None:
            kw["bias"] = bias
        if scale is not None:
            kw["scale"] = scale
        if accum is not None:
            kw["accum_out"] = accum
        P.add("act", lambda e: e.activation(out, in_, func, **kw), r, w, cost=fcost("act", out) * (1.6 if "scale" in kw and not isinstance(kw["scale"], float) else 1.0))

    def tt(eng, out, a, b, op, r, w):
        P.add(eng, lambda e: e.tensor_tensor(out, a, b, op), r, w, cost=fcost(eng, out))

    def ts(eng, out, a, s1, s2, op0, op1, r, w):
        if s2 is None:
            P.add(eng, lambda e: e.tensor_scalar(out, a, s1, None, op0), r, w, cost=fcost(eng, out))
        else:
            P.add(eng, lambda e: e.tensor_scalar(out, a, s1, s2, op0, op1), r, w, cost=fcost(eng, out))

    def stt(eng, out, a, s, b, op0, op1, r, w):
        P.add(eng, lambda e: e.scalar_tensor_tensor(out, a, s, b, op0, op1), r, w, cost=fcost(eng, out))

    def cp(eng, out, in_, r, w):
        if eng == "act":
            P.add("act", lambda e: e.copy(out, in_), r, w, cost=fcost("act", out))
        else:
            P.add(eng, lambda e: e.tensor_copy(out, in_), r, w, cost=fcost(eng, out))

    def red(out, in_, r, w):
        P.add("dve", lambda e: e.tensor_reduce(out, in_, AX.X, ALU.add), r, w, cost=fcost("dve", in_))

    def rsqrt_(t, key, mul, add):
        act(t, t, AF.Ln, [key, "epsc"], [key], bias=epsc[:, 0:1], scale=float(mul))
        act(t, t, AF.Exp, [key], [key], scale=-0.5)

    P.dma(id_f[:], ident, w=["id_f"]); P.dma(id_b[:], ident, w=["id_b"], eng="pool")
    P.dma(mk[:, 0, :], mprev, w=["mk"], eng="pool"); P.dma(mk[:, 1, :], mnext, w=["mk"], eng="pool")
    P.dma(negb[:], negr, w=["negb"], eng="pool")
    P.dma(valt[:], vala, w=["valt"]); P.dma(flg[:], flags, w=["flg"])
    P.add("pool", lambda e: e.memset(ones_f[:], 1.0), w=["ones_f"])
    P.add("pool", lambda e: e.memset(ones_c[:], 1.0), w=["ones_c"])
    P.add("pool", lambda e: e.memset(epsc[:], EPS), w=["epsc"])
    for pb in range(4):
        P.dma(XRp[:, pb, :], xp[pb * 128:(pb + 1) * 128, :], w=[f"XRp{pb}"])
    for sbk in range(1, 15):
        P.dma(XRs[:, sbk - 1, :], xs[sbk * 128:(sbk + 1) * 128, :], w=[f"XRs{sbk}"])

    def load_T(dst, src_rows, nrows, key):
        tmp = carve(128, F32)
        P.dma(tmp[0:nrows, :], src_rows, w=[key + "_ld"])
        tr(pq[0][:, 0:nrows], tmp[0:nrows, :], id_f[0:nrows, 0:nrows], [key + "_ld", "id_f"], ["pq0"])
        cp("dve", dst, pq[0][:, 0:nrows], ["pq0"], [key])

    load_T(cT[:], cond, 16, "cT")
    load_T(nwT[:], nw, 32, "nwT")
    for l in range(2):
        load_T(badaT[:, l * 48:(l + 1) * 48], bada[l], 48, "badaT")
        load_T(cwT[:, l, 0:128], convw[l, 0:128, :], 128, "cwT")
        load_T(cwT[:, l, 128:132], convw[l, 128:132, :], 4, "cwT")
        load_T(cbT[:, l, :], convb[l], 44, "cbT")
    act(scT[:], cT[:], AF.Silu, ["cT"], ["scT"])
    P.dma(esink[:], sink[0, :].partition_broadcast(128), w=["esink"])
    act(esink[:], esink[:], AF.Exp, ["esink"], ["esink"])

    WS = P.sb([128, 8192], BF16, "WS")
    wst = [WS[:, i * 4096:(i + 1) * 4096].rearrange("p (k n) -> p k n", k=8) for i in range(2)]
    wrot = Rot([0, 1])

    SOFF = [4]

    def gblk(stream, b):
        return b if stream == 0 else SOFF[0] + b

    def xres(stream, b):
        if stream == 0:
            return XRp[:, b, :], f"XRp{b}"
        return XRs[:, b - 1, :], f"XRs{b}"

    def adaln(L):
        modT = modTs[L]; AB = ABs[L]
        P.mark(f"L{L}-adaln")
        for piece in range(12):
            wi = wrot.get()
            wt = wst[wi]
            P.dma(wt[:], wada[L].rearrange("(kc p) n -> p kc n", p=128)[:, :, piece * 512:(piece + 1) * 512],
                  w=[f"wst{wi}"], eng="pool")
            for oc4 in range(4):
                oc = piece * 4 + oc4
                for kc in range(8):
                    mm(tpbF[:, 2 * oc:2 * oc + 2], wt[:, kc, oc4 * 128:(oc4 + 1) * 128], scT[:, 2 * kc:2 * kc + 2],
                       start=(oc4 == 0 and kc == 0), stop=(kc == 7), r=[f"wst{wi}", "scT"], w=["tpb"], skip=True)
            tt("dve", modT[:, piece * 4:(piece + 1) * 4, :], tpbF[:, 8 * piece:8 * piece + 8].rearrange("p (o c) -> p o c", c=2),
               badaT[:, L * 48 + piece * 4:L * 48 + (piece + 1) * 4].unsqueeze(2).broadcast_to([128, 4, 2]), ALU.add,
               ["tpb", "badaT"], [f"modT{L}p{piece}"])
        for c in range(2):
            for s_ in range(2):
                base = s_ * 24
                mk_scale = [f"modT{L}p{(base + 8) // 4}", f"modT{L}p{(base + 12) // 4}"]
                mk_shift = [f"modT{L}p{base // 4}", f"modT{L}p{(base + 4) // 4}"]
                abk = f"AB{L}{s_}"
                ts("dve", AB[:, c, 2 * s_, :], modT[:, base + 8:base + 16, c], 1.0, None, ALU.add, None, mk_scale, [abk])
                tt("dve", AB[:, c, 2 * s_, :], AB[:, c, 2 * s_, :], nwT[:, L * 16 + s_ * 8:L * 16 + s_ * 8 + 8], ALU.mult,
                   [abk, "nwT"], [abk])
                cp("dve", AB[:, c, 2 * s_ + 1, :], modT[:, base:base + 8, c], mk_shift, [abk])

    def layer(L):
        isA = (L == 0)
        chk(1 + 10 * L)
        nqk = 6 if isA else 8
        ncols = 448 if isA else 768
        nk = 1 if isA else 4
        vcol0 = nqk * 64
        arena_reset()
        modT = modTs[L]; AB = ABs[L]

        def build_gates(s_):
            dgr = Rot([(carve(128, F32), f"dg{i}") for i in range(2)])
            for c in range(2):
                for j in range(8):
                    dg, dgk = dgr.get()
                    ts("dve", dg, id_f[:], modT[:, s_ * 24 + 16 + j, c:c + 1], None, ALU.mult, None,
                       ["id_f", f"modT{L}p{(s_ * 24 + 16) // 4}", f"modT{L}p{(s_ * 24 + 20) // 4}"], [dgk])
                    mm(pq[0][:, (j % 4) * 128:(j % 4 + 1) * 128], ones_f[:], dg, True, True, ["ones_f", dgk], ["pq0"])
                    cp("act", Gb[:, c, j * 128:(j + 1) * 128], pq[0][:, (j % 4) * 128:(j % 4 + 1) * 128], ["pq0"], [f"Gb{c}"])

        if isA:
            s_norm = list(range(0, 16)); s_q = list(range(1, 15))
            ffn_tiles = [(0, [0, 1]), (0, [2, 3]), (1, [1, 2, 3]), (1, [4, 5, 6]), (1, [7, 8, 9]), (1, [10, 11, 12]), (1, [13, 14])]
        else:
            s_norm = list(range(1, 14)); s_q = list(range(3, 13))
            ffn_tiles = [(0, [0, 1]), (0, [2, 3]), (1, [4, 5, 6]), (1, [7, 8, 9]), (1, [10, 11])]
        blocks = [(0, b) for b in range(4)] + [(1, b) for b in s_norm]
        qset = set([(0, b) for b in range(4)] + [(1, b) for b in s_q])

        xn_rot = Rot([(carve(D, BF16), f"xn{i}") for i in range(2 if isA else 1)])
        xtmp = carve(D, F32) if isA else None
        st_rot = Rot([(carve(8, F32), f"st{i}") for i in range(4)])

        def norm_block(xap, xkey):
            xn, xnk = xn_rot.get()
            st, stk = st_rot.get()
            act(xn, xap, AF.Square, [xkey], [xnk, stk], accum=st[:, 0:1])
            rsqrt_(st[:, 0:1], stk, 1.0 / D, EPS)
            ts("dve", xn, xap, st[:, 0:1], None, ALU.mult, None, [xkey, stk], [xnk])
            return xn, xnk

        NB = 20 if isA else 17
        SOFF[0] = 4 if isA else 3
        hT = carve(8 * NB * 128, BF16).rearrange("p (k t) -> p k t", k=8)

        def norm_to_hT(stream, b, which):
            if stream == 1 and (b == 0 or b == 15):
                P.dma(xtmp, xs[b * 128:(b + 1) * 128, :], w=["xtmp"])
                xap, xkey = xtmp, "xtmp"
            else:
                xap, xkey = xres(stream, b)
            xn, xnk = norm_block(xap, xkey)
            g = gblk(stream, b)
            for kc in range(8):
                tr(tpb[:, kc * 128:(kc + 1) * 128], xn[:, kc * 128:(kc + 1) * 128], id_b[:], [xnk, "id_b"], ["tpb"])
            for kc in range(8):
                dst = hT[:, kc, g * 128:(g + 1) * 128]
                a_ = AB[:, stream, 2 * which, kc:kc + 1]; b_ = AB[:, stream, 2 * which + 1, kc:kc + 1]
                if kc % 2 == 0:
                    act(dst, tpb[:, kc * 128:(kc + 1) * 128], AF.Identity, ["tpb", f"AB{L}0"], [f"hT{g}"], bias=b_, scale=a_)
                else:
                    ts("dve", dst, tpb[:, kc * 128:(kc + 1) * 128], a_, b_, ALU.mult, ALU.add, ["tpb", f"AB{L}0"], [f"hT{g}"])

        chk(2 + 10 * L)
        P.mark(f"L{L}-norm")
        for (stream, b) in blocks:
            norm_to_hT(stream, b, 0)
        build_gates(0)

        nkc = 1 if isA else 2
        QT = carve(2 * NB * 128, BF16).rearrange("p (k t) -> p k t", k=2)
        KT = carve(nkc * NB * 128, BF16).rearrange("p (k t) -> p k t", k=nkc)
        VA = carve(NB * nk * 65, BF16).rearrange("p (b h d) -> p b h d", b=NB, h=nk)
        Wg = WS[:, 0:8 * 768].rearrange("p (k n) -> p k n", k=8)
        Wog = carve(2 * D, BF16).rearrange("p (k n) -> p k n", k=2)
        KcT = carve(nkc * 512, BF16).rearrange("p (k t) -> p k t", k=nkc)
        Vc = carve(4 * nk * 65, BF16).rearrange("p (b h d) -> p b h d", b=4, h=nk)
        cstb = carve(4 * 256, BF16).rearrange("p (b n) -> p b n", b=4)
        T_rot = Rot([(carve(768, F32), f"T{i}") for i in range(3 if isA else 2)])
        SQ_rot = Rot([(carve(512, F32), f"SQ{i}") for i in range(2)])
        R1 = carve(384, F32) if isA else None
        rst_rot = Rot([(carve(8, F32), f"rs{i}") for i in range(3)])
        QKb_rot = Rot([(carve(512, BF16), f"QKb{i}") for i in range(3 if isA else 2)])
        PT_rot = Rot([(carve(512, BF16), f"PT{i}") for i in range(6 if isA else 4)])
        OS_rot = Rot([(carve(256, BF16), f"OS{i}") for i in range(3 if isA else 2)])
        OT_rot = Rot([(carve(256, BF16).rearrange("p (c t) -> p c t", c=2), f"OT{i}") for i in range(3 if isA else 2)])
        den_rot = Rot([(carve(8, F32), f"den{i}") for i in range(3)])
        rC = carve(64, F32) if isA else None
        rS = carve(64, F32) if isA else None
        if not isA:
            BT = carve(4 * 7 * 128, BF16).rearrange("p (h e k) -> p h e k", h=4, e=7)
            BTf = carve(7 * 128, BF16).rearrange("p (e k) -> p e k", e=7)
            cmk = carve(128, BF16)
            inv_rot = Rot([(carve(6 * 128, BF16), f"inv{i}") for i in range(1)])
            P.dma(cmk, cmask, w=["cmk"], eng="pool")
        Spairs = [((Sps[0], Sps[1]), ("S0", "S1"), S2), ((yps[0], yps[1]), ("y0", "y1"), Y2)]
        wo_d = woa if isA else wob
        wg_d = wga if isA else wgb
        ck_d = cka if isA else ckb
        cv_d = cva if isA else cvb
        nk_d = nka if isA else nkb
        nv_d = nva if isA else nvb
        P.dma(qkw[:, 0:nqk * 64], (qkwa if isA else qkwb)[0, :].partition_broadcast(128), w=["qkw"])
        P.add("pool", lambda e: e.memset(qkw[:, nqk * 64:ncols], 1.0), w=["qkw"])

        for g in range(4):
            P.mark(f"L{L}-g{g}-proj")
            for kh in range(2):
                P.dma(Wg[:, kh * 4:(kh + 1) * 4, 0:ncols],
                      wg_d[g].rearrange("(kc p) n -> p kc n", p=128)[:, kh * 4:(kh + 1) * 4, :], w=["wst0", "wst1"], eng="pool")
            if not isA:
                for hh in range(4):
                    P.dma(BTf.rearrange("p e k -> p (e k)"), btab[g][:, hh * 896:(hh + 1) * 896], w=["BTf"], eng="pool")
                    tt("dve", BTf, BTf, cmk.unsqueeze(1).broadcast_to([128, 7, 128]), ALU.add, ["BTf", "cmk"], ["BTf"])
                    ts("dve", BT[:, hh, :, :], BTf, 1.0 / SCALE, None, ALU.mult, None, ["BTf"], ["BT"])
            kw_ = 64 if isA else 256
            ckv = ck_d.rearrange("(b p) n -> p b n", p=128)[:, :, g * kw_:(g + 1) * kw_]
            if isA:
                for cpy in range(2):
                    P.dma(cstb[:, :, cpy * 64:(cpy + 1) * 64], ckv, w=["cstb"], eng="pool")
            else:
                P.dma(cstb[:, :, 0:256], ckv, w=["cstb"], eng="pool")
            for kb in range(4):
                for c in range(nkc):
                    tr(tpb[:, c * 128:(c + 1) * 128], cstb[:, kb, c * 128:(c + 1) * 128], id_b[:], ["cstb", "id_b"], ["tpb"])
                for c in range(nkc):
                    cp("act", KcT[:, c, kb * 128:(kb + 1) * 128], tpb[:, c * 128:(c + 1) * 128], ["tpb"], ["KcT"])
            for kb in range(4):
                P.dma(Vc[:, kb, 0:nk, 0:64], cv_d[kb * 128:(kb + 1) * 128, g * kw_:(g + 1) * kw_].rearrange("p (h d) -> p h d", d=64),
                      w=["Vc"], eng="pool")
            P.add("pool", lambda e: e.memset(Vc[:, :, 0:nk, 64:65], 1.0), w=["Vc"])

            def proj_block(stream, b):
                gb = gblk(stream, b)
                T, Tk = T_rot.get()
                for kc in range(8):
                    mm(pq[0][:, 0:min(ncols, 512)], hT[:, kc, gb * 128:(gb + 1) * 128], Wg[:, kc, 0:min(ncols, 512)],
                       kc == 0, kc == 7, [f"hT{gb}", "wst0", "wst1"], ["pq0"])
                nq = nqk * 64
                SQ, SQk = SQ_rot.get()
                act(SQ[:, 0:min(nq, 512)], pq[0][:, 0:min(nq, 512)], AF.Square, ["pq0"], [SQk])
                tt("dve", T[:, 0:min(ncols, 512)], pq[0][:, 0:min(ncols, 512)], qkw[:, 0:min(ncols, 512)], ALU.mult, ["pq0", "qkw"], [Tk])
                if ncols > 512:
                    for kc in range(8):
                        mm(pq[0][:, 0:ncols - 512], hT[:, kc, gb * 128:(gb + 1) * 128], Wg[:, kc, 512:ncols],
                           kc == 0, kc == 7, [f"hT{gb}", "wst0", "wst1"], ["pq0"])
                    tt("dve", T[:, 512:ncols], pq[0][:, 0:ncols - 512], qkw[:, 512:ncols], ALU.mult, ["pq0", "qkw"], [Tk])
                rst, rsk = rst_rot.get()
                red(rst[:, 0:nqk], SQ[:, 0:nq].rearrange("p (h d) -> p h d", d=64), [SQk], [rsk])
                rsqrt_(rst[:, 0:nqk], rsk, 1.0 / HD, EPS)
                QN = T
                tt("dve", QN[:, 0:nq].rearrange("p (h d) -> p h d", d=64), T[:, 0:nq].rearrange("p (h d) -> p h d", d=64),
                   rst[:, 0:nqk].unsqueeze(2).broadcast_to([128, nqk, 64]), ALU.mult, [Tk, rsk], [Tk])
                if stream == 0:
                    rows = slice(b * 128, (b + 1) * 128)
                    kcs = 64 if isA else 256
                    P.dma(nk_d[rows, g * kcs:(g + 1) * kcs], QN[:, 256:256 + kcs], r=[Tk])
                    P.dma(nv_d[rows, g * kcs:(g + 1) * kcs], T[:, vcol0:vcol0 + kcs], r=[Tk])
                QKb, QKk = QKb_rot.get()
                if isA and stream == 1:
                    P.dma(rC, ropeC[b * 128:(b + 1) * 128, :], w=["rC"])
                    P.dma(rS, ropeS[b * 128:(b + 1) * 128, :], w=["rS"])
                    q5 = QN[:, 0:nq].rearrange("p (h a t d) -> p h a t d", a=2, t=2, d=16)
                    r5 = R1[:, 0:nq].rearrange("p (h a t d) -> p h a t d", a=2, t=2, d=16)
                    s5 = rS.rearrange("p (a t d) -> p a t d", a=2, t=2)
                    for t_ in range(2):
                        tt("dve", r5[:, :, :, t_, :], q5[:, :, :, 1 - t_, :],
                           s5[:, :, t_, :].unsqueeze(1).broadcast_to([128, nqk, 2, 16]), ALU.mult, [Tk, "rS"], ["R1"])
                    tt("pool", QN[:, 0:nq].rearrange("p (h d) -> p h d", d=64), QN[:, 0:nq].rearrange("p (h d) -> p h d", d=64),
                       rC.unsqueeze(1).broadcast_to([128, nqk, 64]), ALU.mult, [Tk, "rC"], [Tk])
                    tt("dve", QKb[:, 0:nq], QN[:, 0:nq], R1[:, 0:nq], ALU.add, [Tk, "R1"], [QKk])
                else:
                    cp("act", QKb[:, 0:nq], QN[:, 0:nq], [Tk], [QKk])
                vv = valt[:, b:b + 1] if stream == 1 else ones_c[:, 0:1]
                ts("dve", VA[:, gb, 0:nk, 0:64], T[:, vcol0:vcol0 + nk * 64].rearrange("p (h d) -> p h d", d=64), vv, None, ALU.mult, None,
                   [Tk, "valt", "ones_c"], [f"VA{gb}"])
                for h_ in range(nk):
                    cp("pool", VA[:, gb, h_, 64:65], vv, ["valt", "ones_c"], [f"VA{gb}"])
                nch = nq // 128
                for c in range(nch):
                    tr(tpb[:, c * 128:(c + 1) * 128], QKb[:, c * 128:(c + 1) * 128], id_b[:], [QKk, "id_b"], ["tpb"])
                for c in range(nch):
                    if c < 2:
                        cp("act" if c == 0 else "dve", QT[:, c, gb * 128:(gb + 1) * 128], tpb[:, c * 128:(c + 1) * 128], ["tpb"], [f"QT{gb}"])
                    else:
                        cp("act" if c == 2 else "dve", KT[:, c - 2, gb * 128:(gb + 1) * 128], tpb[:, c * 128:(c + 1) * 128], ["tpb"], [f"KT{gb}"])

            cur_stream = [None]

            def attn_block(stream, b):
                if (stream, b) not in qset or stream not in ATT_STREAMS:
                    return
                if stream != cur_stream[0]:
                    cur_stream[0] = stream
                    P.dma(Wog[:], wo_d[g * 256:(g + 1) * 256, :].rearrange("(kc p) n -> p kc n", p=128), w=["Wog"], eng="pool")
                    tt("pool", Wog[:, :, :], Wog[:, :, :], Gb[:, stream, :].unsqueeze(1).broadcast_to([128, 2, D]), ALU.mult,
                       ["Wog", f"Gb{stream}"], ["Wog"])
                gb = gblk(stream, b)
                kl = []
                if stream == 0:
                    s0 = (b // 2) * 2
                    for kb_ in (s0, s0 + 1):
                        kl.append(("loc", kb_, None, None, None))
                else:
                    if isA:
                        for kb_, mi in ((b - 1, 0), (b, None), (b + 1, 1)):
                            kl.append(("loc", SOFF[0] + kb_, mi, None, None))
                    else:
                        lo, hi = B_KLIST[b]
                        for kb_ in range(lo, hi + 1):
                            kl.append(("loc", SOFF[0] + kb_, None, kb_ - b + 3, B_UNITS[(b, kb_)]))
                    for kb_ in range(4):
                        kl.append(("ctx", kb_, None, None, None))
                if stream == 1 and not isA:
                    lo, hi = B_KLIST[b]
                    u0 = B_UNITS[(b, lo)]; nu = hi - lo + 1
                    invq, invk = inv_rot.get()
                    P.dma(invq[0:2, 0:nu * 128], invt[:, u0 * 128:(u0 + nu) * 128], w=[invk], eng="pool")
                for ki, (kind, kb_, mi, be, iu) in enumerate(kl):
                    Sb2, Sk2, Sfull = Spairs[ki % 2]
                    for hh in (0, 2, 1, 3):
                        hs = slice((hh % 2) * 64, (hh % 2) * 64 + 64)
                        kc_ = 0 if isA else hh // 2
                        if kind == "loc":
                            kap = KT[hs, kc_, kb_ * 128:(kb_ + 1) * 128]; kkey = f"KT{kb_}"
                        else:
                            kap = KcT[hs, kc_, kb_ * 128:(kb_ + 1) * 128]; kkey = "KcT"
                        last = (be is None)
                        j2 = hh // 2
                        mm(Sb2[hh % 2][:, j2 * 128:(j2 + 1) * 128], kap, QT[hs, hh // 2, gb * 128:(gb + 1) * 128],
                           j2 == 0, last, [kkey, f"QT{gb}"], [Sk2[hh % 2]], skip=True)
                    if be is not None:
                        for hh in (0, 2, 1, 3):
                            j2 = hh // 2
                            mm(Sb2[hh % 2][:, j2 * 128:(j2 + 1) * 128], BT[:, hh, be, :], id_b[:], False, False, ["BT", "id_b"], [Sk2[hh % 2]], skip=True)
                        for half in range(2):
                            mm(Sb2[half][:, 0:256].rearrange("p (h q) -> p h q", h=2), negb[0:2, :],
                               invq[0:2, (iu - u0) * 128:(iu - u0 + 1) * 128].unsqueeze(1).broadcast_to([2, 2, 128]), False, True,
                               ["negb", invk], [Sk2[half]], skip=True)
                    PT, PTk = PT_rot.get()
                    act(PT.rearrange("p (b c) -> p b c", b=2), Sfull[:, :].rearrange("p (b c) -> p b c", b=2)[:, :, 0:256], AF.Exp,
                        [Sk2[0], Sk2[1]], [PTk], scale=SCALE)
                    if ATT_SUB < 2:
                        continue
                    if mi is not None:
                        tt("dve", PT.rearrange("p (h q) -> p h q", h=4), PT.rearrange("p (h q) -> p h q", h=4),
                           mk[:, mi, :].unsqueeze(1).broadcast_to([128, 4, 128]), ALU.mult, [PTk, "mk"], [PTk])
                    for hh in range(4):
                        vh = 0 if isA else hh
                        if kind == "loc":
                            vap = VA[:, kb_, vh, :]; vkey = f"VA{kb_}"
                        else:
                            vap = Vc[:, kb_, vh, :]; vkey = "Vc"
                        pbi = (hh % 2) * 2 + hh // 2
                        mm(Ops[:, hh * 65:(hh + 1) * 65], PT[:, pbi * 128:(pbi + 1) * 128], vap,
                           (ki == 0 and hh == 0), ki == len(kl) - 1, [PTk, vkey], ["Ops"], skip=True)
                if ATT_SUB < 3:
                    return
                den, dk = den_rot.get()
                o3 = Ops[:, 0:260].rearrange("p (h d) -> p h d", d=65)
                if isA:
                    tt("dve", den[:, 0:4], o3[:, :, 64], esink[:, g * 4:(g + 1) * 4], ALU.add, ["Ops", "esink"], [dk])
                else:
                    cp("dve", den[:, 0:4], o3[:, :, 64], ["Ops"], [dk])
                P.add("dve", lambda e, den=den: e.reciprocal(den[:, 0:4], den[:, 0:4]), [dk], [dk])
                OS, OSk = OS_rot.get()
                tt("dve", OS.rearrange("p (h d) -> p h d", d=64), o3[:, :, 0:64],
                   den[:, 0:4].unsqueeze(2).broadcast_to([128, 4, 64]), ALU.mult, ["Ops", dk], [OSk])
                if ATT_SUB < 4:
                    return
                OT, OTk = OT_rot.get()
                for c in range(2):
                    tr(tpb[:, 512 + c * 128:512 + (c + 1) * 128], OS[:, c * 128:(c + 1) * 128], id_b[:], [OSk, "id_b"], ["tpb"])
                cp("act", OT, tpb[:, 512:768].rearrange("p (c t) -> p c t", c=2), ["tpb"], [OTk])
                if ATT_SUB < 5:
                    return
                xap, xkey = xres(stream, b)
                for half in range(2):
                    for c in range(2):
                        mm(pq[1][:, :], OT[:, c, :], Wog[:, c, half * 512:(half + 1) * 512], c == 0, c == 1,
                           [OTk, "Wog"], ["pq1"])
                    tt("dve", xap[:, half * 512:(half + 1) * 512], pq[1][:, :], xap[:, half * 512:(half + 1) * 512], ALU.add,
                       ["pq1", xkey], [xkey])

            pos = {blk: i for i, blk in enumerate(blocks)}
            qlist = [blk for blk in blocks if blk in qset]

            def last_needed(blk):
                st_, b_ = blk
                if st_ == 0:
                    return pos[(0, (b_ // 2) * 2 + 1)]
                if isA:
                    return pos[(1, min(b_ + 1, 15))]
                return pos[(1, B_KLIST[b_][1])]
            qi = 0
            for idx, (stream, b) in enumerate(blocks):
                proj_block(stream, b)
                while INTERLEAVE(g) and qi < len(qlist) and last_needed(qlist[qi]) <= idx:
                    attn_block(*qlist[qi])
                    qi += 1
            while qi < len(qlist):
                attn_block(*qlist[qi])
                qi += 1
        chk(6 + 10 * L)
        P.mark(f"L{L}-ffn")
        arena_reset()
        build_gates(1)
        s_first, s_last = (1, 14) if isA else (3, 12)
        nS = s_last - s_first + 1
        hTp = carve(8 * 516, BF16).rearrange("p (k t) -> p k t", k=8)
        hTs = carve(8 * (nS * 128 + 2), BF16).rearrange("p (k t) -> p k t", k=8)
        NJ = 3
        WU = [carve(8 * 2 * NJ * 128, BF16).rearrange("p (k n) -> p k n", k=8) for _ in range(2)]
        WD = [carve(NJ * D, BF16).rearrange("p (j n) -> p j n", j=NJ) for _ in range(2)]
        WDs = [carve(NJ * D, BF16).rearrange("p (j n) -> p j n", j=NJ) for _ in range(2)]
        aT_rot = Rot([(carve(NJ * 384, BF16).rearrange("p (j t) -> p j t", j=NJ), f"aT{i}") for i in range(3)])
        xn_rot = Rot([(carve(D, BF16), f"xn{i}") for i in range(1)])
        st_rot = Rot([(carve(8, F32), f"st{i}") for i in range(4)])
        cv_rot = Rot([(carve(384, F32), f"cv{i}") for i in range(6)])
        up_banks = Rot([(pq[0], "pq0"), (pq[1], "pq1"), (Ops, "Ops")])
        dn_pairs = Rot([(Y2, ("y0", "y1")), (S2, ("S0", "S1"))])
        P.add("pool", lambda e: e.memset(hTp[:, :, :], 0.0), w=["hTp"])
        P.add("pool", lambda e: e.memset(hTs[:, :, 0:1], 0.0), w=["hTs_e"])
        P.add("pool", lambda e: e.memset(hTs[:, :, nS * 128 + 1:nS * 128 + 2], 0.0), w=["hTs_e"])
        for stream, blist in ((0, [0, 1, 2, 3]), (1, list(range(s_first, s_last + 1)))):
            for b in blist:
                xap, xkey = xres(stream, b)
                xn, xnk = norm_block(xap, xkey)
                for kc in range(8):
                    tr(tpb[:, kc * 128:(kc + 1) * 128], xn[:, kc * 128:(kc + 1) * 128], id_b[:], [xnk, "id_b"], ["tpb"])
                if stream == 0:
                    c0 = (b // 2) * 258 + 1 + (b % 2) * 128; hk = "hTp"
                    dst3 = hTp
                else:
                    c0 = 1 + (b - s_first) * 128; hk = f"hTs{b}"
                    dst3 = hTs
                for kc in range(8):
                    dst = dst3[:, kc, c0:c0 + 128]
                    a_ = AB[:, stream, 2, kc:kc + 1]; b_ = AB[:, stream, 3, kc:kc + 1]
                    if kc % 2 == 0:
                        act(dst, tpb[:, kc * 128:(kc + 1) * 128], AF.Identity, ["tpb", f"AB{L}1"], [hk], bias=b_, scale=a_)
                    else:
                        ts("dve", dst, tpb[:, kc * 128:(kc + 1) * 128], a_, b_, ALU.mult, ALU.add, ["tpb", f"AB{L}1"], [hk])
        for fi, u in enumerate((511, 1536)):
            bu = u // 128
            if s_first <= bu <= s_last:
                c0 = 1 + u - s_first * 128
                ts("dve", hTs[:, :, c0], hTs[:, :, c0], flg[:, fi:fi + 1], None, ALU.mult, None, [f"hTs{bu}", "flg"], [f"hTs{bu}"])
        tiles = [(0, [0, 1], hTp, 0, ["hTp"]), (0, [2, 3], hTp, 258, ["hTp"])]
        fb = s_first if isA else 4
        lb = 13 if isA else 11
        b = fb
        while b <= lb:
            tb = list(range(b, min(b + 3, lb + 1)))
            keys = [f"hTs{x}" for x in range(max(tb[0] - 1, s_first), min(tb[-1] + 1, s_last) + 1)] + ["hTs_e"]
            tiles.append((1, tb, hTs, (tb[0] - s_first) * 128, keys))
            b += 3
        wupv = wup[L].rearrange("(kc p) n -> p kc n", p=128)
        wdnv = wdn[L].rearrange("(j p) n -> p j n", p=128)
        groups = [(0, 3), (3, 3), (6, 3), (9, 3), (12, 3), (15, 3), (18, 2), (20, 2)]
        for gi, (j0, nj) in enumerate(groups):
            if FFN_SUB < 2:
                break
            wb = gi % 2
            P.dma(WU[wb][:, :, 0:nj * 128], wupv[:, :, j0 * 128:(j0 + nj) * 128], w=[f"WU{wb}"], eng="pool")
            P.dma(WU[wb][:, :, NJ * 128:NJ * 128 + nj * 128], wupv[:, :, DFF + j0 * 128:DFF + (j0 + nj) * 128], w=[f"WU{wb}"], eng="pool")
            P.dma(WD[wb][:, 0:nj, :], wdnv[:, j0:j0 + nj, :], w=[f"WD{wb}"], eng="pool")
            cur_c = None
            for (stream, tb, hsrc, cb, hkeys) in tiles:
                if stream != cur_c:
                    cur_c = stream
                    tt("pool", WDs[wb][:, 0:nj, :], WD[wb][:, 0:nj, :],
                       Gb[:, stream, :].unsqueeze(1).broadcast_to([128, nj, D]), ALU.mult, [f"WD{wb}", f"Gb{stream}"], [f"WDs{wb}"])
                if FFN_SUB < 3:
                    continue
                n = len(tb) * 128
                n1 = n + 2
                aT, aTk = aT_rot.get()
                for jj in range(nj):
                    cvs = []
                    for isval in range(2):
                        ch = j0 + jj + 22 * isval
                        wc = isval * NJ * 128 + jj * 128
                        bank, bk = up_banks.get()
                        for kc in range(8):
                            mm(bank[:, 0:n1], WU[wb][:, kc, wc:wc + 128], hsrc[:, kc, cb:cb + n1], kc == 0, kc == 7, [f"WU{wb}"] + hkeys, [bk])
                        cv, cvk = cv_rot.get()
                        w0 = cwT[:, L, ch:ch + 1]; w1 = cwT[:, L, 44 + ch:44 + ch + 1]; w2 = cwT[:, L, 88 + ch:88 + ch + 1]
                        act(cv[:, 0:n], bank[:, 1:n + 1], AF.Identity, [bk, "cwT", "cbT"], [cvk], bias=cbT[:, L, ch:ch + 1], scale=w1)
                        stt("dve", cv[:, 0:n], bank[:, 0:n], w0, cv[:, 0:n], ALU.mult, ALU.add, [bk, cvk, "cwT"], [cvk])
                        stt("dve", cv[:, 0:n], bank[:, 2:n + 2], w2, cv[:, 0:n], ALU.mult, ALU.add, [bk, cvk, "cwT"], [cvk])
                        cvs.append((cv, cvk))
                    if FFN_SUB < 4:
                        continue
                    (gcv, gk), (vcv, vk) = cvs
                    act(gcv[:, 0:n], gcv[:, 0:n], AF.Silu, [gk], [gk])
                    tt("pool", aT[:, jj, 0:n], gcv[:, 0:n], vcv[:, 0:n], ALU.mult, [gk, vk], [aTk])
                for bi, b in enumerate(tb):
                    if FFN_SUB < 5:
                        continue
                    xap, xkey = xres(stream, b)
                    ypair, ypk = dn_pairs.get()
                    for half in range(2):
                        for jj in range(nj):
                            mm(ypair[:, half * 512:(half + 1) * 512], aT[:, jj, bi * 128:(bi + 1) * 128],
                               WDs[wb][:, jj, half * 512:(half + 1) * 512], jj == 0, jj == nj - 1, [aTk, f"WDs{wb}"], [ypk[half]])
                    tt("dve", xap[:, :], ypair[:, :], xap[:, :], ALU.add, [ypk[0], ypk[1], xkey], [xkey])

    try:
        adaln(0)
        for L in range(2):
            layer(L)
            if L == 0:
                adaln(1)
            chk(7 + 10 * L)
    except StopBuild:
        pass
    P.mark("out")
    for pb in range(4):
        P.dma(yp[pb * 128:(pb + 1) * 128, :], XRp[:, pb, :], r=[f"XRp{pb}"])
    for i, sbk in enumerate(range(4, 12)):
        P.dma(ys[i * 128:(i + 1) * 128, :], XRs[:, sbk - 1, :], r=[f"XRs{sbk}"])
    P.emit()
    return nc, P.stats


def _consts():
    ident = np.eye(128, dtype=np.float32)
    kl = np.arange(128)[:, None]; ql = np.arange(128)[None, :]
    mprev = (kl >= ql).astype(np.float32)
    mnext = (kl <= ql).astype(np.float32)
    col = np.arange(64)
    cs = np.clip(col - 8, 0, 48)
    col_ok = (col[None, :] >= cs[:, None]) & (col[None, :] < cs[:, None] + 16)
    cm = np.where(col_ok, 0.0, NEG).astype(np.float32)
    cmask = np.tile(cm, (2, 2))
    negr = np.zeros((2, 128), np.float32)
    negr[0, 0:64] = NEG / SCALE
    negr[1, 64:128] = NEG / SCALE
    return ident, mprev, mnext, cmask, negr


def _rope_tables(t):
    half = 16
    freqs = (10000.0 ** (-np.arange(half, dtype=np.float32) / np.float32(half))).astype(np.float32)
    row = (t // 64).astype(np.float32); colp = (t % 64).astype(np.float32)
    ar = row[:, None] * freqs[None, :]; ac = colp[:, None] * freqs[None, :]
    cr, sr, cc, sc = np.cos(ar), np.sin(ar), np.cos(ac), np.sin(ac)
    C = np.concatenate([cr, cr, cc, cc], axis=1).astype(np.float32)
    S = np.concatenate([-sr, sr, -sc, sc], axis=1).astype(np.float32)
    return C, S


def make_in_maps(x_prompt, x_sample, cache_k_a, cache_v_a, cache_k_b, cache_v_b, c, c_ctx,
                 norm_attn_w, norm_ffn_w, w_ada, b_ada, w_qkv_a, q_norm_a, k_norm_a, sink_a, w_o_a,
                 w_qkv_b, q_norm_b, k_norm_b, rpb_b, w_o_b, w_up, conv_w, conv_b, w_down):
    f = np.float32
    x_prompt = np.asarray(x_prompt, f); x_sample = np.asarray(x_sample, f)
    ident, mprev, mnext, cmask, negr = _consts()
    wqa = np.asarray(w_qkv_a, f)[0]; wqb = np.asarray(w_qkv_b, f)[0]
    wga = np.stack([np.concatenate([wqa[:, g * 256:(g + 1) * 256], wqa[:, 1024 + g * 64:1024 + (g + 1) * 64],
                                    wqa[:, 1024 + g * 64:1024 + (g + 1) * 64], wqa[:, 1280 + g * 64:1280 + (g + 1) * 64]], axis=1)
                    for g in range(4)])
    wgb = np.stack([np.concatenate([wqb[:, g * 256:(g + 1) * 256], wqb[:, 1024 + g * 256:1024 + (g + 1) * 256],
                                    wqb[:, 2048 + g * 256:2048 + (g + 1) * 256]], axis=1) for g in range(4)])
    qa = np.asarray(q_norm_a, f)[0]; ka = np.asarray(k_norm_a, f)[0]
    qb_ = np.asarray(q_norm_b, f)[0]; kb_ = np.asarray(k_norm_b, f)[0]
    qkwa = np.concatenate([np.tile(qa, 4), np.tile(ka, 2)])[None, :]
    qkwb = np.concatenate([np.tile(qb_, 4), np.tile(kb_, 4)])[None, :]
    nw = np.stack([np.asarray(norm_attn_w, f), np.asarray(norm_ffn_w, f)], axis=1).reshape(32, 128)
    bada = np.asarray(b_ada, f).reshape(2, 48, 128)
    convw = np.asarray(conv_w, f).reshape(2, 3, 44, 128).reshape(2, 132, 128)
    convb = np.asarray(conv_b, f).reshape(2, 44, 128)
    rpb = np.asarray(rpb_b, f)[0]
    bq = np.arange(128) // 64; qc = np.arange(128) % 64
    ak = np.arange(128) // 64; kc = np.arange(128) % 64
    dc = np.clip(kc[None, :] - qc[:, None], -15, 15) + 15
    btab = np.zeros((4, 128, 4, 7, 128), f)
    for e in range(7):
        dr = np.clip(2 * (e - 3) + ak[None, :] - bq[:, None], -7, 7) + 7
        for g in range(4):
            for hh in range(4):
                btab[g, :, hh, e, :] = rpb[g * 4 + hh][dr, dc]
    btab = btab.reshape(4, 128, 4 * 7 * 128)
    shared = dict(wada=np.asarray(w_ada, f), wga=wga, wgb=wgb, woa=np.asarray(w_o_a, f)[0], wob=np.asarray(w_o_b, f)[0],
                  wup=np.asarray(w_up, f), wdn=np.asarray(w_down, f), qkwa=qkwa, qkwb=qkwb, sink=np.asarray(sink_a, f),
                  nw=nw, bada=bada, convw=convw, convb=convb, ident=ident, mprev=mprev, mnext=mnext,
                  btab=btab, cmask=cmask, negr=negr)
    in_maps = []
    for core in range(8):
        bs = core // 4; qt = core % 4; t0 = qt * 1024; R0 = qt * 16
        t = np.arange(NSLAB * 128) + t0 - 512
        ok = (t >= 0) & (t < DEC_SEQ)
        xs = np.zeros((NSLAB * 128, D), f)
        xs[ok] = x_sample[bs, t[ok]]
        C, S = _rope_tables(t)
        vala = ok.astype(f).reshape(NSLAB, 128).T.copy()
        flags = np.ones((128, 2), f)
        if qt == 0:
            flags[:, 0] = 0.0
        if qt == 3:
            flags[:, 1] = 0.0
        inv = np.zeros((2, NUNIT, 2, 64), f)
        for (qb, kb), u in B_UNITS.items():
            for a in range(2):
                kr = R0 - 8 + 2 * kb + a
                for b in range(2):
                    r = R0 - 8 + 2 * qb + b
                    rs = min(max(r - 4, 0), 56)
                    valid = (0 <= kr < 64) and (rs <= kr < rs + 8)
                    inv[a, u, b, :] = 0.0 if valid else 1.0
        cond = np.stack([np.asarray(c_ctx, f).reshape(8, 128), np.asarray(c, f)[bs].reshape(8, 128)], axis=1).reshape(16, 128)
        m = dict(shared)
        m.update(xp=x_prompt[2 * core:2 * core + 2].reshape(512, D), xs=xs, cond=cond,
                 cka=np.asarray(cache_k_a, f)[bs, 0].reshape(PAST, 256), cva=np.asarray(cache_v_a, f)[bs, 0].reshape(PAST, 256),
                 ckb=np.asarray(cache_k_b, f)[bs, 0].reshape(PAST, D), cvb=np.asarray(cache_v_b, f)[bs, 0].reshape(PAST, D),
                 ropeC=C, ropeS=S, vala=vala, flags=flags, invt=inv.reshape(2, NUNIT * 128))
        in_maps.append({k: np.ascontiguousarray(v, dtype=f) for k, v in m.items()})
    return in_maps


def kernel(**inputs):
    f = np.float32
    in_maps = make_in_maps(**inputs)
    nc, stats = build()
    res = run_bass_kernel_spmd(nc, in_maps, core_ids=list(range(8)))
    R = res.results
    y_p = np.stack([R[cq]["yp"].reshape(2, SEQ, D) for cq in range(8)]).reshape(16, SEQ, D)
    y_s = np.stack([np.concatenate([R[b * 4 + q]["ys"] for q in range(4)], axis=0) for b in range(2)])
    nka = np.concatenate([R[cq]["nka"].reshape(2, 1, SEQ, 4, HD) for cq in range(8)], axis=0)
    nva = np.concatenate([R[cq]["nva"].reshape(2, 1, SEQ, 4, HD) for cq in range(8)], axis=0)
    nkb = np.concatenate([R[cq]["nkb"].reshape(2, 1, SEQ, 16, HD) for cq in range(8)], axis=0)
    nvb = np.concatenate([R[cq]["nvb"].reshape(2, 1, SEQ, 16, HD) for cq in range(8)], axis=0)
    return (y_p.astype(f), y_s.astype(f), nka.astype(f), nva.astype(f), nkb.astype(f), nvb.astype(f))
```

```python
from contextlib import ExitStack
import numpy as np
import concourse.bass as bass
import concourse.mybir as mybir
from concourse.bass_utils import run_bass_kernel_spmd

F32 = mybir.dt.float32
BF16 = mybir.dt.bfloat16
AF = mybir.ActivationFunctionType
ALU = mybir.AluOpType
AX = mybir.AxisListType

ENGS = ("pe", "act", "dve", "pool", "sp")
NDMA = {"sp": 24, "pool": 6}
NROT = 4
PSUM_KEYS = ("pq0", "pq1", "tpb", "S0", "S1", "Ops", "y0", "y1")
SCHED = True
FUSE_WAIT = True
VERBOSE = False
PRIO_MODE = 1
PRIO_Q = 2000.0
INTERLEAVE = lambda g: False
WAITDBG = None


class Prog:
    def __init__(self, nc):
        self.nc = nc
        self.ops = []
        self.stack = ExitStack()
        self.n_sb = 0

    def sb(self, shape, dt, name=None):
        self.n_sb += 1
        return self.stack.enter_context(self.nc.sbuf_tensor(name or f"sb{self.n_sb}", list(shape), dt))

    def ps(self, shape, dt, name=None):
        self.n_sb += 1
        return self.stack.enter_context(self.nc.psum_tensor(name or f"ps{self.n_sb}", list(shape), dt))

    def mark(self, name):
        if not hasattr(self, "marks"):
            self.marks = []
        self.marks.append((name, len(self.ops)))

    def barrier(self, scratch):
        self.ops.append(["pool", lambda e: e.memset(scratch, 0.0), (), ("__bar__",), False, True, 100.0])

    def add(self, eng, fn, r=(), w=(), dma=False, cost=None):
        pr = [k for k in r if k in PSUM_KEYS]
        if pr:
            r = [k for k in r if k not in PSUM_KEYS]
            w = list(w) + pr
        if cost is None:
            cost = {"pe": 100.0, "act": 400.0, "dve": 300.0, "pool": 500.0, "sp": 2500.0}[eng]
        self.ops.append([eng, fn, tuple(r) + ("__bar__",), tuple(w), dma, False, float(cost)])

    def dma(self, out, in_, r=(), w=(), eng="sp", **kw):
        try:
            nbytes = out.free_size() * 4 * 128
        except Exception:
            nbytes = 65536
        self.add(eng, lambda e: e.dma_start(out=out, in_=in_, **kw), r, w, dma=True, cost=2200.0 + nbytes / 150.0)

    def schedule(self, deps):
        import heapq
        ops = self.ops
        n = len(ops)
        succ = [[] for _ in range(n)]
        indeg = [0] * n
        for i in range(n):
            indeg[i] = len(deps[i])
            for d in deps[i]:
                succ[d].append(i)
        prio = list(range(n))
        if PRIO_MODE == 1:
            bl = [0.0] * n
            for i in range(n - 1, -1, -1):
                m = 0.0
                for j in succ[i]:
                    if bl[j] > m:
                        m = bl[j]
                bl[i] = m + ops[i][6]
            prio = [(-int(bl[i] / PRIO_Q), i) for i in range(n)]
        ready = {e: [] for e in ENGS}
        for i in range(n):
            if indeg[i] == 0:
                heapq.heappush(ready[ops[i][0]], (prio[i], i))
        busy = {e: 0.0 for e in ENGS}
        events = []
        order = []
        now = 0.0
        SYNC = 150.0
        self.t_start = [0.0] * n
        self.t_fin = [0.0] * n
        rel = [0.0] * n
        while len(order) < n:
            progressed = False
            for e in ENGS:
                if busy[e] <= now and ready[e]:
                    _p, i = heapq.heappop(ready[e])
                    c = ops[i][6]
                    if ops[i][4]:
                        issue = 1000.0 if e == "pool" else 60.0
                        busy[e] = now + issue
                        fin = now + issue + c
                    else:
                        busy[e] = now + c
                        fin = now + c
                    heapq.heappush(events, (fin, i))
                    order.append(i)
                    self.t_start[i] = now
                    self.t_fin[i] = fin
                    progressed = True
            if progressed:
                continue
            cand = [busy[e] for e in ENGS if busy[e] > now and ready[e]]
            tn = min(cand) if cand else None
            if events and (tn is None or events[0][0] <= tn):
                t, i = heapq.heappop(events)
                now = max(now, t)
                if i < 0:
                    j = -i - 1
                    heapq.heappush(ready[ops[j][0]], (prio[j], j))
                    continue
                ei = ops[i][0]
                for j in succ[i]:
                    ej = ops[j][0]
                    lat = 0.0 if (ei == "pe" and ej == "pe") else (70.0 if ei == ej else SYNC)
                    rel[j] = max(rel[j], t + lat)
                    indeg[j] -= 1
                    if indeg[j] == 0:
                        if rel[j] <= now:
                            heapq.heappush(ready[ej], (prio[j], j))
                        else:
                            heapq.heappush(events, (rel[j], -j - 1))
            else:
                assert tn is not None, "scheduler stuck"
                now = tn
        self.model_span = now
        if WAITDBG:
            t0, t1 = WAITDBG
            eng_ops = {e: [] for e in ENGS}
            for i in order:
                eng_ops[ops[i][0]].append(i)
            for e in ENGS:
                prev_end = 0.0
                wait_by = {}
                busy_t = 0.0
                for i in eng_ops[e]:
                    st_ = self.t_start[i]
                    if t0 <= st_ < t1:
                        gap = st_ - prev_end
                        if gap > 1.0 and deps[i]:
                            crit = max(deps[i], key=lambda d: self.t_fin[d])
                            kk = ops[crit][0] + ("-dma" if ops[crit][4] else "")
                            wait_by[kk] = wait_by.get(kk, 0.0) + gap
                        busy_t += (self.t_fin[i] - st_) if not ops[i][4] else 0.0
                    prev_end = max(prev_end, self.t_fin[i] if not ops[i][4] else st_ + 60.0)
                print(f"   [{e}] busy {busy_t/1e3:7.1f}us of {(t1-t0)/1e3:.1f}; waits on:", {k: round(v / 1e3, 1) for k, v in wait_by.items()})
        if VERBOSE and getattr(self, "marks", None):
            mk_ = self.marks + [("end", n)]
            for (nm, a), (_, b_) in zip(mk_[:-1], mk_[1:]):
                if b_ > a:
                    pe_b = sum(ops[i][6] for i in range(a, b_) if ops[i][0] == "pe")
                    ac_b = sum(ops[i][6] for i in range(a, b_) if ops[i][0] == "act")
                    dv_b = sum(ops[i][6] for i in range(a, b_) if ops[i][0] == "dve")
                    po_b = sum(ops[i][6] for i in range(a, b_) if ops[i][0] == "pool" and not ops[i][4])
                    print(f"  phase {nm:12s} ops {b_-a:6d} t=[{min(self.t_start[a:b_])/1e3:8.1f},{max(self.t_fin[a:b_])/1e3:8.1f}]us pe={pe_b/1e3:7.1f} act={ac_b/1e3:7.1f} dve={dv_b/1e3:7.1f} pool={po_b/1e3:7.1f}")
        bs = {e: 0.0 for e in ENGS}
        for o in ops:
            bs[o[0]] += (60.0 if o[4] and o[0] == "sp" else 1000.0 if o[4] else o[6])
        self.model_busy = bs
        return order

    def emit(self):
        nc = self.nc
        ops = self.ops
        n = len(ops)
        last_w = {}
        readers = {}
        deps = [None] * n
        last_bar = -1
        for i, (eng, fn, r, w, isd, isbar, _c) in enumerate(ops):
            d = set()
            if isbar:
                d.update(range(last_bar + 1, i))
                last_bar = i
            for k in r:
                if k in last_w:
                    d.add(last_w[k])
            for k in w:
                if k in last_w:
                    d.add(last_w[k])
                if k != "__bar__":
                    for x in readers.get(k, ()):
                        d.add(x)
            d.discard(i)
            deps[i] = d
            for k in w:
                last_w[k] = i
                readers[k] = []
            for k in r:
                if k not in w:
                    readers.setdefault(k, []).append(i)
        if SCHED:
            order = self.schedule(deps)
            pos = {o: p for p, o in enumerate(order)}
            ops = [ops[o] for o in order]
            deps = [set(pos[d] for d in deps[o]) for o in order]
            self.ops = ops
        need_sig = [False] * n
        fdeps = [None] * n
        for i in range(n):
            eng = ops[i][0]
            best = {}
            dm = []
            for d in deps[i]:
                if ops[d][4]:
                    dm.append(d)
                else:
                    e2 = ops[d][0]
                    if e2 == "pe" and eng == "pe" and not ops[i][4]:
                        continue
                    if e2 not in best or best[e2] < d:
                        best[e2] = d
            fdeps[i] = (best, dm)
        seen = {e: {} for e in ENGS}
        seen_d = {e: set() for e in ENGS}
        clk = [None] * n
        for i in range(n):
            eng = ops[i][0]
            best, dm = fdeps[i]
            nb = {}
            for e2, d in sorted(best.items(), key=lambda kv: -kv[1]):
                if seen[eng].get(e2, -1) >= d:
                    continue
                nb[e2] = d
                need_sig[d] = True
                sd = seen[eng]
                if sd.get(e2, -1) < d:
                    sd[e2] = d
                if clk[d] is not None:
                    for e3, v3 in clk[d].items():
                        if sd.get(e3, -1) < v3:
                            sd[e3] = v3
            nd = [d for d in dm if d not in seen_d[eng]]
            for d in nd:
                seen_d[eng].add(d)
            fdeps[i] = (nb, nd)
            if not ops[i][4]:
                clk[i] = dict(seen[eng])
        sems = {}
        for e in ("pe", "act", "dve", "pool"):
            sems[e] = [self.stack.enter_context(nc.semaphore(f"c_{e}{j}")) for j in range(NROT)]
        dsems = {}
        for e in ("sp", "pool"):
            dsems[e] = [self.stack.enter_context(nc.semaphore(f"d_{e}{j}")) for j in range(NDMA[e])]
        sig = [None] * n
        cnt = {e: 0 for e in ENGS}
        dcnt = {e: 0 for e in ENGS}
        prev_dma = [None] * n
        for i in range(n):
            eng, fn, r, w, isd, _b, _c = ops[i]
            if isd:
                j = dcnt[eng]
                dcnt[eng] += 1
                nd_ = NDMA[eng]
                sig[i] = (dsems[eng][j % nd_], 16 * (j // nd_ + 1), 16)
                if j >= nd_:
                    prev_dma[i] = (dsems[eng][j % nd_], 16 * (j // nd_))
            elif need_sig[i]:
                j = cnt[eng]
                cnt[eng] += 1
                sig[i] = (sems[eng][j % NROT], j // NROT + 1, 1)
        self.stats = dict(n=n, span=getattr(self, "model_span", None), busy=getattr(self, "model_busy", None), sig=dict(cnt), dma=dict(dcnt),
                          per_eng={e: sum(1 for o in ops if o[0] == e) for e in ENGS})
        per = {e: [i for i in range(n) if ops[i][0] == e] for e in ENGS}

        def run(e_name, eobj):
            for i in per[e_name]:
                eng, fn, r, w, isd, _b, _c = ops[i]
                nb, nd = fdeps[i]
                waits = [prev_dma[i]] if prev_dma[i] is not None else []
                waits += [sig[d][:2] for d in sorted(list(nb.values()) + list(nd))]
                for s_, v_ in (waits[:-1] if FUSE_WAIT else waits):
                    eobj.wait_ge(s_, v_)
                ins = fn(eobj)
                if FUSE_WAIT and waits:
                    ins._wait_ge(*waits[-1])
                if sig[i] is not None:
                    ins.then_inc(sig[i][0], sig[i][2])
            j = dcnt[e_name]
            if j > 0:
                nd_ = NDMA[e_name]
                for q in range(min(j, nd_)):
                    uses = (j - 1 - q) // nd_ + 1
                    eobj.wait_ge(dsems[e_name][q], 16 * uses)

        with nc.Block() as block:
            @block.tensor
            def _(e):
                run("pe", e)

            @block.scalar
            def _(e):
                run("act", e)

            @block.vector
            def _(e):
                run("dve", e)

            @block.gpsimd
            def _(e):
                run("pool", e)

            @block.sync
            def _(e):
                run("sp", e)
        self.stack.close()


D = 1024
SEQ = 256
DEC_SEQ = 4096
PAST = 512
HD = 64
DFF = 2816
EPS = 1e-6
SCALE = HD ** -0.5
NEG = -30000.0
NSLAB = 16
STAGE = 99
ATT_SUB = 9
FFN_SUB = 9
ATT_STREAMS = (0, 1)


class StopBuild(Exception):
    pass


def chk(k):
    if k > STAGE:
        raise StopBuild()

B_KLIST = {3: (1, 5), 4: (2, 7), 5: (3, 7), 6: (4, 8), 7: (5, 9), 8: (6, 10), 9: (7, 11),
           10: (8, 12), 11: (8, 13), 12: (10, 13)}
B_QBLKS = list(range(3, 13))
B_UNITS = {}
for _q in B_QBLKS:
    for _k in range(B_KLIST[_q][0], B_KLIST[_q][1] + 1):
        B_UNITS[(_q, _k)] = len(B_UNITS)
NUNIT = len(B_UNITS)


class Rot:
    def __init__(self, items):
        self.items = items
        self.i = 0

    def get(self):
        x = self.items[self.i % len(self.items)]
        self.i += 1
        return x


def build():
    nc = bass.Bass("TRN2", target_bir_lowering=False)

    def din(name, shape):
        return nc.dram_tensor(name, list(shape), F32, kind="ExternalInput").ap()

    def dout(name, shape):
        return nc.dram_tensor(name, list(shape), F32, kind="ExternalOutput").ap()

    xp = din("xp", [512, D]); xs = din("xs", [NSLAB * 128, D]); cond = din("cond", [16, 128])
    nw = din("nw", [32, 128]); bada = din("bada", [2, 48, 128])
    wada = din("wada", [2, D, 6 * D]); wga = din("wga", [4, D, 448]); wgb = din("wgb", [4, D, 768])
    woa = din("woa", [D, D]); wob = din("wob", [D, D]); wup = din("wup", [2, D, 2 * DFF]); wdn = din("wdn", [2, DFF, D])
    qkwa = din("qkwa", [1, 384]); qkwb = din("qkwb", [1, 512]); sink = din("sink", [1, 16])
    convw = din("convw", [2, 132, 128]); convb = din("convb", [2, 44, 128])
    cka = din("cka", [PAST, 256]); cva = din("cva", [PAST, 256]); ckb = din("ckb", [PAST, D]); cvb = din("cvb", [PAST, D])
    ident = din("ident", [128, 128]); mprev = din("mprev", [128, 128]); mnext = din("mnext", [128, 128])
    ropeC = din("ropeC", [NSLAB * 128, 64]); ropeS = din("ropeS", [NSLAB * 128, 64])
    vala = din("vala", [128, NSLAB]); flags = din("flags", [128, 2])
    btab = din("btab", [4, 128, 4 * 7 * 128]); cmask = din("cmask", [128, 128])
    invt = din("invt", [2, NUNIT * 128]); negr = din("negr", [2, 128])
    yp = dout("yp", [512, D]); ys = dout("ys", [1024, D])
    nka = dout("nka", [512, 256]); nva = dout("nva", [512, 256]); nkb = dout("nkb", [512, D]); nvb = dout("nvb", [512, D])

    P = Prog(nc)
    pq = [P.ps([128, 512], F32, f"pq{i}") for i in range(2)]
    tpbF = P.ps([128, 512], F32, "tpbF")
    tpb = tpbF[:, :].bitcast(BF16)
    S2 = P.ps([128, 1024], F32, "S2")
    Sps = [S2[:, 0:512], S2[:, 512:1024]]
    Ops = P.ps([128, 512], F32, "Ops")
    Y2 = P.ps([128, 1024], F32, "Y2")
    yps = [Y2[:, 0:512], Y2[:, 512:1024]]

    XRp = P.sb([128, 4, D], F32, "XRp")
    XRs = P.sb([128, 14, D], F32, "XRs")
    Gb = P.sb([128, 2, D], F32, "Gb")
    id_f = P.sb([128, 128], F32, "id_f"); id_b = P.sb([128, 128], BF16, "id_b")
    ones_f = P.sb([128, 128], F32, "ones_f")
    mk = P.sb([128, 2, 128], BF16, "mk")
    cT = P.sb([128, 16], F32, "cT"); scT = P.sb([128, 16], BF16, "scT")
    nwT = P.sb([128, 32], F32, "nwT"); badaT = P.sb([128, 96], F32, "badaT")
    cwT = P.sb([128, 2, 132], F32, "cwT"); cbT = P.sb([128, 2, 44], F32, "cbT")
    modTs = [P.sb([128, 48, 2], F32, f"modT{l}") for l in range(2)]
    ABs = [P.sb([128, 2, 4, 8], F32, f"AB{l}") for l in range(2)]
    qkw = P.sb([128, 768], F32, "qkw"); esink = P.sb([128, 16], F32, "esink")
    valt = P.sb([128, NSLAB], F32, "valt"); ones_c = P.sb([128, 1], F32, "ones_c"); flg = P.sb([128, 2], F32, "flg")
    negb = P.sb([2, 128], BF16, "negb")
    epsc = P.sb([128, 1], F32, "epsc")
    scr = P.sb([128, 8], F32, "scr")
    ARENA_F = 26368
    arena = P.sb([128, ARENA_F], F32, "arena")
    aoff = [0]

    def carve(nelem, dt):
        words = nelem if dt == F32 else (nelem + 1) // 2
        words = (words + 7) // 8 * 8
        a = arena[:, aoff[0]:aoff[0] + words]
        aoff[0] += words
        assert aoff[0] <= ARENA_F, aoff[0]
        return a if dt == F32 else a.bitcast(BF16)[:, 0:nelem]

    def arena_reset():
        if VERBOSE:
            print("arena used words", aoff[0], "of", ARENA_F)
        P.barrier(scr[:, 0:1])
        aoff[0] = 0

    def mm(out, lhsT, rhs, start, stop, r, w, skip=False):
        P.add("pe", lambda e: e.matmul(out, lhsT, rhs, start=start, stop=stop, skip_group_check=skip), r, w,
              cost=35.0 + max(rhs.free_size(), 64) / 2.4)

    def tr(out, in_, idm, r, w):
        P.add("pe", lambda e: e.transpose(out, in_, idm), r, w, cost=90.0)

    def fcost(eng, out):
        f = out.free_size()
        return {"act": 230.0 + f / 1.1, "dve": 90.0 + f / 0.8, "pool": 150.0 + f / 0.42}[eng]

    def act(out, in_, func, r, w, bias=None, scale=None, accum=None):
        kw = {}
        if bias is not None:
            kw["bias"] = bias
        if scale is not None:
            kw["scale"] = scale
        if accum is not None:
            kw["accum_out"] = accum
        P.add("act", lambda e: e.activation(out, in_, func, **kw), r, w, cost=fcost("act", out) * (1.6 if "scale" in kw and not isinstance(kw["scale"], float) else 1.0))

    def tt(eng, out, a, b, op, r, w):
        P.add(eng, lambda e: e.tensor_tensor(out, a, b, op), r, w, cost=fcost(eng, out))

    def ts(eng, out, a, s1, s2, op0, op1, r, w):
        if s2 is None:
            P.add(eng, lambda e: e.tensor_scalar(out, a, s1, None, op0), r, w, cost=fcost(eng, out))
        else:
            P.add(eng, lambda e: e.tensor_scalar(out, a, s1, s2, op0, op1), r, w, cost=fcost(eng, out))

    def stt(eng, out, a, s, b, op0, op1, r, w):
        P.add(eng, lambda e: e.scalar_tensor_tensor(out, a, s, b, op0, op1), r, w, cost=fcost(eng, out))

    def cp(eng, out, in_, r, w):
        if eng == "act":
            P.add("act", lambda e: e.copy(out, in_), r, w, cost=fcost("act", out))
        else:
            P.add(eng, lambda e: e.tensor_copy(out, in_), r, w, cost=fcost(eng, out))

    def red(out, in_, r, w):
        P.add("dve", lambda e: e.tensor_reduce(out, in_, AX.X, ALU.add), r, w, cost=fcost("dve", in_))

    def rsqrt_(t, key, mul, add):
        act(t, t, AF.Ln, [key, "epsc"], [key], bias=epsc[:, 0:1], scale=float(mul))
        act(t, t, AF.Exp, [key], [key], scale=-0.5)

    P.dma(id_f[:], ident, w=["id_f"]); P.dma(id_b[:], ident, w=["id_b"], eng="pool")
    P.dma(mk[:, 0, :], mprev, w=["mk"], eng="pool"); P.dma(mk[:, 1, :], mnext, w=["mk"], eng="pool")
    P.dma(negb[:], negr, w=["negb"], eng="pool")
    P.dma(valt[:], vala, w=["valt"]); P.dma(flg[:], flags, w=["flg"])
    P.add("pool", lambda e: e.memset(ones_f[:], 1.0), w=["ones_f"])
    P.add("pool", lambda e: e.memset(ones_c[:], 1.0), w=["ones_c"])
    P.add("pool", lambda e: e.memset(epsc[:], EPS), w=["epsc"])
    for pb in range(4):
        P.dma(XRp[:, pb, :], xp[pb * 128:(pb + 1) * 128, :], w=[f"XRp{pb}"])
    for sbk in range(1, 15):
        P.dma(XRs[:, sbk - 1, :], xs[sbk * 128:(sbk + 1) * 128, :], w=[f"XRs{sbk}"])

    def load_T(dst, src_rows, nrows, key):
        tmp = carve(128, F32)
        P.dma(tmp[0:nrows, :], src_rows, w=[key + "_ld"])
        tr(pq[0][:, 0:nrows], tmp[0:nrows, :], id_f[0:nrows, 0:nrows], [key + "_ld", "id_f"], ["pq0"])
        cp("dve", dst, pq[0][:, 0:nrows], ["pq0"], [key])

    load_T(cT[:], cond, 16, "cT")
    load_T(nwT[:], nw, 32, "nwT")
    for l in range(2):
        load_T(badaT[:, l * 48:(l + 1) * 48], bada[l], 48, "badaT")
        load_T(cwT[:, l, 0:128], convw[l, 0:128, :], 128, "cwT")
        load_T(cwT[:, l, 128:132], convw[l, 128:132, :], 4, "cwT")
        load_T(cbT[:, l, :], convb[l], 44, "cbT")
    act(scT[:], cT[:], AF.Silu, ["cT"], ["scT"])
    P.dma(esink[:], sink[0, :].partition_broadcast(128), w=["esink"])
    act(esink[:], esink[:], AF.Exp, ["esink"], ["esink"])

    WS = P.sb([128, 8192], BF16, "WS")
    wst = [WS[:, i * 4096:(i + 1) * 4096].rearrange("p (k n) -> p k n", k=8) for i in range(2)]
    wrot = Rot([0, 1])

    SOFF = [4]

    def gblk(stream, b):
        return b if stream == 0 else SOFF[0] + b

    def xres(stream, b):
        if stream == 0:
            return XRp[:, b, :], f"XRp{b}"
        return XRs[:, b - 1, :], f"XRs{b}"

    def adaln(L):
        modT = modTs[L]; AB = ABs[L]
        P.mark(f"L{L}-adaln")
        for piece in range(12):
            wi = wrot.get()
            wt = wst[wi]
            P.dma(wt[:], wada[L].rearrange("(kc p) n -> p kc n", p=128)[:, :, piece * 512:(piece + 1) * 512],
                  w=[f"wst{wi}"], eng="pool")
            for oc4 in range(4):
                oc = piece * 4 + oc4
                for kc in range(8):
                    mm(tpbF[:, 2 * oc:2 * oc + 2], wt[:, kc, oc4 * 128:(oc4 + 1) * 128], scT[:, 2 * kc:2 * kc + 2],
                       start=(oc4 == 0 and kc == 0), stop=(kc == 7), r=[f"wst{wi}", "scT"], w=["tpb"], skip=True)
            tt("dve", modT[:, piece * 4:(piece + 1) * 4, :], tpbF[:, 8 * piece:8 * piece + 8].rearrange("p (o c) -> p o c", c=2),
               badaT[:, L * 48 + piece * 4:L * 48 + (piece + 1) * 4].unsqueeze(2).broadcast_to([128, 4, 2]), ALU.add,
               ["tpb", "badaT"], [f"modT{L}p{piece}"])
        for c in range(2):
            for s_ in range(2):
                base = s_ * 24
                mk_scale = [f"modT{L}p{(base + 8) // 4}", f"modT{L}p{(base + 12) // 4}"]
                mk_shift = [f"modT{L}p{base // 4}", f"modT{L}p{(base + 4) // 4}"]
                abk = f"AB{L}{s_}"
                ts("dve", AB[:, c, 2 * s_, :], modT[:, base + 8:base + 16, c], 1.0, None, ALU.add, None, mk_scale, [abk])
                tt("dve", AB[:, c, 2 * s_, :], AB[:, c, 2 * s_, :], nwT[:, L * 16 + s_ * 8:L * 16 + s_ * 8 + 8], ALU.mult,
                   [abk, "nwT"], [abk])
                cp("dve", AB[:, c, 2 * s_ + 1, :], modT[:, base:base + 8, c], mk_shift, [abk])

    def layer(L):
        isA = (L == 0)
        chk(1 + 10 * L)
        nqk = 6 if isA else 8
        ncols = 448 if isA else 768
        nk = 1 if isA else 4
        vcol0 = nqk * 64
        arena_reset()
        modT = modTs[L]; AB = ABs[L]

        def build_gates(s_):
            dgr = Rot([(carve(128, F32), f"dg{i}") for i in range(2)])
            for c in range(2):
                for j in range(8):
                    dg, dgk = dgr.get()
                    ts("dve", dg, id_f[:], modT[:, s_ * 24 + 16 + j, c:c + 1], None, ALU.mult, None,
                       ["id_f", f"modT{L}p{(s_ * 24 + 16) // 4}", f"modT{L}p{(s_ * 24 + 20) // 4}"], [dgk])
                    mm(pq[0][:, (j % 4) * 128:(j % 4 + 1) * 128], ones_f[:], dg, True, True, ["ones_f", dgk], ["pq0"])
                    cp("act", Gb[:, c, j * 128:(j + 1) * 128], pq[0][:, (j % 4) * 128:(j % 4 + 1) * 128], ["pq0"], [f"Gb{c}"])

        if isA:
            s_norm = list(range(0, 16)); s_q = list(range(1, 15))
            ffn_tiles = [(0, [0, 1]), (0, [2, 3]), (1, [1, 2, 3]), (1, [4, 5, 6]), (1, [7, 8, 9]), (1, [10, 11, 12]), (1, [13, 14])]
        else:
            s_norm = list(range(1, 14)); s_q = list(range(3, 13))
            ffn_tiles = [(0, [0, 1]), (0, [2, 3]), (1, [4, 5, 6]), (1, [7, 8, 9]), (1, [10, 11])]
        blocks = [(0, b) for b in range(4)] + [(1, b) for b in s_norm]
        qset = set([(0, b) for b in range(4)] + [(1, b) for b in s_q])

        xn_rot = Rot([(carve(D, BF16), f"xn{i}") for i in range(2 if isA else 1)])
        xtmp = carve(D, F32) if isA else None
        st_rot = Rot([(carve(8, F32), f"st{i}") for i in range(4)])

        def norm_block(xap, xkey):
            xn, xnk = xn_rot.get()
            st, stk = st_rot.get()
            act(xn, xap, AF.Square, [xkey], [xnk, stk], accum=st[:, 0:1])
            rsqrt_(st[:, 0:1], stk, 1.0 / D, EPS)
            ts("dve", xn, xap, st[:, 0:1], None, ALU.mult, None, [xkey, stk], [xnk])
            return xn, xnk

        NB = 20 if isA else 17
        SOFF[0] = 4 if isA else 3
        hT = carve(8 * NB * 128, BF16).rearrange("p (k t) -> p k t", k=8)

        def norm_to_hT(stream, b, which):
            if stream == 1 and (b == 0 or b == 15):
                P.dma(xtmp, xs[b * 128:(b + 1) * 128, :], w=["xtmp"])
                xap, xkey = xtmp, "xtmp"
            else:
                xap, xkey = xres(stream, b)
            xn, xnk = norm_block(xap, xkey)
            g = gblk(stream, b)
            for kc in range(8):
                tr(tpb[:, kc * 128:(kc + 1) * 128], xn[:, kc * 128:(kc + 1) * 128], id_b[:], [xnk, "id_b"], ["tpb"])
            for kc in range(8):
                dst = hT[:, kc, g * 128:(g + 1) * 128]
                a_ = AB[:, stream, 2 * which, kc:kc + 1]; b_ = AB[:, stream, 2 * which + 1, kc:kc + 1]
                if kc % 2 == 0:
                    act(dst, tpb[:, kc * 128:(kc + 1) * 128], AF.Identity, ["tpb", f"AB{L}0"], [f"hT{g}"], bias=b_, scale=a_)
                else:
                    ts("dve", dst, tpb[:, kc * 128:(kc + 1) * 128], a_, b_, ALU.mult, ALU.add, ["tpb", f"AB{L}0"], [f"hT{g}"])

        chk(2 + 10 * L)
        P.mark(f"L{L}-norm")
        for (stream, b) in blocks:
            norm_to_hT(stream, b, 0)
        build_gates(0)

        nkc = 1 if isA else 2
        QT = carve(2 * NB * 128, BF16).rearrange("p (k t) -> p k t", k=2)
        KT = carve(nkc * NB * 128, BF16).rearrange("p (k t) -> p k t", k=nkc)
        VA = carve(NB * nk * 65, BF16).rearrange("p (b h d) -> p b h d", b=NB, h=nk)
        Wg = WS[:, 0:8 * 768].rearrange("p (k n) -> p k n", k=8)
        Wog = carve(2 * D, BF16).rearrange("p (k n) -> p k n", k=2)
        KcT = carve(nkc * 512, BF16).rearrange("p (k t) -> p k t", k=nkc)
        Vc = carve(4 * nk * 65, BF16).rearrange("p (b h d) -> p b h d", b=4, h=nk)
        cstb = carve(4 * 256, BF16).rearrange("p (b n) -> p b n", b=4)
        T_rot = Rot([(carve(768, F32), f"T{i}") for i in range(3 if isA else 2)])
        SQ_rot = Rot([(carve(512, F32), f"SQ{i}") for i in range(2)])
        R1 = carve(384, F32) if isA else None
        rst_rot = Rot([(carve(8, F32), f"rs{i}") for i in range(3)])
        QKb_rot = Rot([(carve(512, BF16), f"QKb{i}") for i in range(3 if isA else 2)])
        PT_rot = Rot([(carve(512, BF16), f"PT{i}") for i in range(6 if isA else 4)])
        OS_rot = Rot([(carve(256, BF16), f"OS{i}") for i in range(3 if isA else 2)])
        OT_rot = Rot([(carve(256, BF16).rearrange("p (c t) -> p c t", c=2), f"OT{i}") for i in range(3 if isA else 2)])
        den_rot = Rot([(carve(8, F32), f"den{i}") for i in range(3)])
        rC = carve(64, F32) if isA else None
        rS = carve(64, F32) if isA else None
        if not isA:
            BT = carve(4 * 7 * 128, BF16).rearrange("p (h e k) -> p h e k", h=4, e=7)
            BTf = carve(7 * 128, BF16).rearrange("p (e k) -> p e k", e=7)
            cmk = carve(128, BF16)
            inv_rot = Rot([(carve(6 * 128, BF16), f"inv{i}") for i in range(1)])
            P.dma(cmk, cmask, w=["cmk"], eng="pool")
        Spairs = [((Sps[0], Sps[1]), ("S0", "S1"), S2), ((yps[0], yps[1]), ("y0", "y1"), Y2)]
        wo_d = woa if isA else wob
        wg_d = wga if isA else wgb
        ck_d = cka if isA else ckb
        cv_d = cva if isA else cvb
        nk_d = nka if isA else nkb
        nv_d = nva if isA else nvb
        P.dma(qkw[:, 0:nqk * 64], (qkwa if isA else qkwb)[0, :].partition_broadcast(128), w=["qkw"])
        P.add("pool", lambda e: e.memset(qkw[:, nqk * 64:ncols], 1.0), w=["qkw"])

        for g in range(4):
            P.mark(f"L{L}-g{g}-proj")
            for kh in range(2):
                P.dma(Wg[:, kh * 4:(kh + 1) * 4, 0:ncols],
                      wg_d[g].rearrange("(kc p) n -> p kc n", p=128)[:, kh * 4:(kh + 1) * 4, :], w=["wst0", "wst1"], eng="pool")
            if not isA:
                for hh in range(4):
                    P.dma(BTf.rearrange("p e k -> p (e k)"), btab[g][:, hh * 896:(hh + 1) * 896], w=["BTf"], eng="pool")
                    tt("dve", BTf, BTf, cmk.unsqueeze(1).broadcast_to([128, 7, 128]), ALU.add, ["BTf", "cmk"], ["BTf"])
                    ts("dve", BT[:, hh, :, :], BTf, 1.0 / SCALE, None, ALU.mult, None, ["BTf"], ["BT"])
            kw_ = 64 if isA else 256
            ckv = ck_d.rearrange("(b p) n -> p b n", p=128)[:, :, g * kw_:(g + 1) * kw_]
            if isA:
                for cpy in range(2):
                    P.dma(cstb[:, :, cpy * 64:(cpy + 1) * 64], ckv, w=["cstb"], eng="pool")
            else:
                P.dma(cstb[:, :, 0:256], ckv, w=["cstb"], eng="pool")
            for kb in range(4):
                for c in range(nkc):
                    tr(tpb[:, c * 128:(c + 1) * 128], cstb[:, kb, c * 128:(c + 1) * 128], id_b[:], ["cstb", "id_b"], ["tpb"])
                for c in range(nkc):
                    cp("act", KcT[:, c, kb * 128:(kb + 1) * 128], tpb[:, c * 128:(c + 1) * 128], ["tpb"], ["KcT"])
            for kb in range(4):
                P.dma(Vc[:, kb, 0:nk, 0:64], cv_d[kb * 128:(kb + 1) * 128, g * kw_:(g + 1) * kw_].rearrange("p (h d) -> p h d", d=64),
                      w=["Vc"], eng="pool")
            P.add("pool", lambda e: e.memset(Vc[:, :, 0:nk, 64:65], 1.0), w=["Vc"])

            def proj_block(stream, b):
                gb = gblk(stream, b)
                T, Tk = T_rot.get()
                for kc in range(8):
                    mm(pq[0][:, 0:min(ncols, 512)], hT[:, kc, gb * 128:(gb + 1) * 128], Wg[:, kc, 0:min(ncols, 512)],
                       kc == 0, kc == 7, [f"hT{gb}", "wst0", "wst1"], ["pq0"])
                nq = nqk * 64
                SQ, SQk = SQ_rot.get()
                act(SQ[:, 0:min(nq, 512)], pq[0][:, 0:min(nq, 512)], AF.Square, ["pq0"], [SQk])
                tt("dve", T[:, 0:min(ncols, 512)], pq[0][:, 0:min(ncols, 512)], qkw[:, 0:min(ncols, 512)], ALU.mult, ["pq0", "qkw"], [Tk])
                if ncols > 512:
                    for kc in range(8):
                        mm(pq[0][:, 0:ncols - 512], hT[:, kc, gb * 128:(gb + 1) * 128], Wg[:, kc, 512:ncols],
                           kc == 0, kc == 7, [f"hT{gb}", "wst0", "wst1"], ["pq0"])
                    tt("dve", T[:, 512:ncols], pq[0][:, 0:ncols - 512], qkw[:, 512:ncols], ALU.mult, ["pq0", "qkw"], [Tk])
                rst, rsk = rst_rot.get()
                red(rst[:, 0:nqk], SQ[:, 0:nq].rearrange("p (h d) -> p h d", d=64), [SQk], [rsk])
                rsqrt_(rst[:, 0:nqk], rsk, 1.0 / HD, EPS)
                QN = T
                tt("dve", QN[:, 0:nq].rearrange("p (h d) -> p h d", d=64), T[:, 0:nq].rearrange("p (h d) -> p h d", d=64),
                   rst[:, 0:nqk].unsqueeze(2).broadcast_to([128, nqk, 64]), ALU.mult, [Tk, rsk], [Tk])
                if stream == 0:
                    rows = slice(b * 128, (b + 1) * 128)
                    kcs = 64 if isA else 256
                    P.dma(nk_d[rows, g * kcs:(g + 1) * kcs], QN[:, 256:256 + kcs], r=[Tk])
                    P.dma(nv_d[rows, g * kcs:(g + 1) * kcs], T[:, vcol0:vcol0 + kcs], r=[Tk])
                QKb, QKk = QKb_rot.get()
                if isA and stream == 1:
                    P.dma(rC, ropeC[b * 128:(b + 1) * 128, :], w=["rC"])
                    P.dma(rS, ropeS[b * 128:(b + 1) * 128, :], w=["rS"])
                    q5 = QN[:, 0:nq].rearrange("p (h a t d) -> p h a t d", a=2, t=2, d=16)
                    r5 = R1[:, 0:nq].rearrange("p (h a t d) -> p h a t d", a=2, t=2, d=16)
                    s5 = rS.rearrange("p (a t d) -> p a t d", a=2, t=2)
                    for t_ in range(2):
                        tt("dve", r5[:, :, :, t_, :], q5[:, :, :, 1 - t_, :],
                           s5[:, :, t_, :].unsqueeze(1).broadcast_to([128, nqk, 2, 16]), ALU.mult, [Tk, "rS"], ["R1"])
                    tt("dve", QN[:, 0:nq].rearrange("p (h d) -> p h d", d=64), QN[:, 0:nq].rearrange("p (h d) -> p h d", d=64),
                       rC.unsqueeze(1).broadcast_to([128, nqk, 64]), ALU.mult, [Tk, "rC"], [Tk])
                    tt("dve", QKb[:, 0:nq], QN[:, 0:nq], R1[:, 0:nq], ALU.add, [Tk, "R1"], [QKk])
                else:
                    cp("act", QKb[:, 0:nq], QN[:, 0:nq], [Tk], [QKk])
                vv = valt[:, b:b + 1] if stream == 1 else ones_c[:, 0:1]
                ts("dve", VA[:, gb, 0:nk, 0:64], T[:, vcol0:vcol0 + nk * 64].rearrange("p (h d) -> p h d", d=64), vv, None, ALU.mult, None,
                   [Tk, "valt", "ones_c"], [f"VA{gb}"])
                for h_ in range(nk):
                    cp("pool", VA[:, gb, h_, 64:65], vv, ["valt", "ones_c"], [f"VA{gb}"])
                nch = nq // 128
                for c in range(nch):
                    tr(tpb[:, c * 128:(c + 1) * 128], QKb[:, c * 128:(c + 1) * 128], id_b[:], [QKk, "id_b"], ["tpb"])
                for c in range(nch):
                    if c < 2:
                        cp("act" if c == 0 else "dve", QT[:, c, gb * 128:(gb + 1) * 128], tpb[:, c * 128:(c + 1) * 128], ["tpb"], [f"QT{gb}"])
                    else:
                        cp("act" if c == 2 else "dve", KT[:, c - 2, gb * 128:(gb + 1) * 128], tpb[:, c * 128:(c + 1) * 128], ["tpb"], [f"KT{gb}"])

            cur_stream = [None]

            def attn_block(stream, b):
                if (stream, b) not in qset or stream not in ATT_STREAMS:
                    return
                if stream != cur_stream[0]:
                    cur_stream[0] = stream
                    P.dma(Wog[:], wo_d[g * 256:(g + 1) * 256, :].rearrange("(kc p) n -> p kc n", p=128), w=["Wog"], eng="pool")
                    tt("pool", Wog[:, :, :], Wog[:, :, :], Gb[:, stream, :].unsqueeze(1).broadcast_to([128, 2, D]), ALU.mult,
                       ["Wog", f"Gb{stream}"], ["Wog"])
                gb = gblk(stream, b)
                kl = []
                if stream == 0:
                    s0 = (b // 2) * 2
                    for kb_ in (s0, s0 + 1):
                        kl.append(("loc", kb_, None, None, None))
                else:
                    if isA:
                        for kb_, mi in ((b - 1, 0), (b, None), (b + 1, 1)):
                            kl.append(("loc", SOFF[0] + kb_, mi, None, None))
                    else:
                        lo, hi = B_KLIST[b]
                        for kb_ in range(lo, hi + 1):
                            kl.append(("loc", SOFF[0] + kb_, None, kb_ - b + 3, B_UNITS[(b, kb_)]))
                    for kb_ in range(4):
                        kl.append(("ctx", kb_, None, None, None))
                if stream == 1 and not isA:
                    lo, hi = B_KLIST[b]
                    u0 = B_UNITS[(b, lo)]; nu = hi - lo + 1
                    invq, invk = inv_rot.get()
                    P.dma(invq[0:2, 0:nu * 128], invt[:, u0 * 128:(u0 + nu) * 128], w=[invk], eng="pool")
                for ki, (kind, kb_, mi, be, iu) in enumerate(kl):
                    Sb2, Sk2, Sfull = Spairs[ki % 2]
                    for hh in (0, 2, 1, 3):
                        hs = slice((hh % 2) * 64, (hh % 2) * 64 + 64)
                        kc_ = 0 if isA else hh // 2
                        if kind == "loc":
                            kap = KT[hs, kc_, kb_ * 128:(kb_ + 1) * 128]; kkey = f"KT{kb_}"
                        else:
                            kap = KcT[hs, kc_, kb_ * 128:(kb_ + 1) * 128]; kkey = "KcT"
                        last = (be is None)
                        j2 = hh // 2
                        mm(Sb2[hh % 2][:, j2 * 128:(j2 + 1) * 128], kap, QT[hs, hh // 2, gb * 128:(gb + 1) * 128],
                           j2 == 0, last, [kkey, f"QT{gb}"], [Sk2[hh % 2]], skip=True)
                    if be is not None:
                        for hh in (0, 2, 1, 3):
                            j2 = hh // 2
                            mm(Sb2[hh % 2][:, j2 * 128:(j2 + 1) * 128], BT[:, hh, be, :], id_b[:], False, False, ["BT", "id_b"], [Sk2[hh % 2]], skip=True)
                        for half in range(2):
                            mm(Sb2[half][:, 0:256].rearrange("p (h q) -> p h q", h=2), negb[0:2, :],
                               invq[0:2, (iu - u0) * 128:(iu - u0 + 1) * 128].unsqueeze(1).broadcast_to([2, 2, 128]), False, True,
                               ["negb", invk], [Sk2[half]], skip=True)
                    PT, PTk = PT_rot.get()
                    act(PT.rearrange("p (b c) -> p b c", b=2), Sfull[:, :].rearrange("p (b c) -> p b c", b=2)[:, :, 0:256], AF.Exp,
                        [Sk2[0], Sk2[1]], [PTk], scale=SCALE)
                    if ATT_SUB < 2:
                        continue
                    if mi is not None:
                        tt("dve", PT.rearrange("p (h q) -> p h q", h=4), PT.rearrange("p (h q) -> p h q", h=4),
                           mk[:, mi, :].unsqueeze(1).broadcast_to([128, 4, 128]), ALU.mult, [PTk, "mk"], [PTk])
                    for hh in range(4):
                        vh = 0 if isA else hh
                        if kind == "loc":
                            vap = VA[:, kb_, vh, :]; vkey = f"VA{kb_}"
                        else:
                            vap = Vc[:, kb_, vh, :]; vkey = "Vc"
                        pbi = (hh % 2) * 2 + hh // 2
                        mm(Ops[:, hh * 65:(hh + 1) * 65], PT[:, pbi * 128:(pbi + 1) * 128], vap,
                           (ki == 0 and hh == 0), ki == len(kl) - 1, [PTk, vkey], ["Ops"], skip=True)
                if ATT_SUB < 3:
                    return
                den, dk = den_rot.get()
                o3 = Ops[:, 0:260].rearrange("p (h d) -> p h d", d=65)
                if isA:
                    tt("dve", den[:, 0:4], o3[:, :, 64], esink[:, g * 4:(g + 1) * 4], ALU.add, ["Ops", "esink"], [dk])
                else:
                    cp("dve", den[:, 0:4], o3[:, :, 64], ["Ops"], [dk])
                P.add("dve", lambda e, den=den: e.reciprocal(den[:, 0:4], den[:, 0:4]), [dk], [dk])
                OS, OSk = OS_rot.get()
                tt("dve", OS.rearrange("p (h d) -> p h d", d=64), o3[:, :, 0:64],
                   den[:, 0:4].unsqueeze(2).broadcast_to([128, 4, 64]), ALU.mult, ["Ops", dk], [OSk])
                if ATT_SUB < 4:
                    return
                OT, OTk = OT_rot.get()
                for c in range(2):
                    tr(tpb[:, 512 + c * 128:512 + (c + 1) * 128], OS[:, c * 128:(c + 1) * 128], id_b[:], [OSk, "id_b"], ["tpb"])
                cp("act", OT, tpb[:, 512:768].rearrange("p (c t) -> p c t", c=2), ["tpb"], [OTk])
                if ATT_SUB < 5:
                    return
                xap, xkey = xres(stream, b)
                for half in range(2):
                    for c in range(2):
                        mm(pq[1][:, :], OT[:, c, :], Wog[:, c, half * 512:(half + 1) * 512], c == 0, c == 1,
                           [OTk, "Wog"], ["pq1"])
                    tt("dve", xap[:, half * 512:(half + 1) * 512], pq[1][:, :], xap[:, half * 512:(half + 1) * 512], ALU.add,
                       ["pq1", xkey], [xkey])

            pos = {blk: i for i, blk in enumerate(blocks)}
            qlist = [blk for blk in blocks if blk in qset]

            def last_needed(blk):
                st_, b_ = blk
                if st_ == 0:
                    return pos[(0, (b_ // 2) * 2 + 1)]
                if isA:
                    return pos[(1, min(b_ + 1, 15))]
                return pos[(1, B_KLIST[b_][1])]
            qi = 0
            for idx, (stream, b) in enumerate(blocks):
                proj_block(stream, b)
                while INTERLEAVE(g) and qi < len(qlist) and last_needed(qlist[qi]) <= idx:
                    attn_block(*qlist[qi])
                    qi += 1
            while qi < len(qlist):
                attn_block(*qlist[qi])
                qi += 1
        chk(6 + 10 * L)
        P.mark(f"L{L}-ffn")
        arena_reset()
        build_gates(1)
        s_first, s_last = (1, 14) if isA else (3, 12)
        nS = s_last - s_first + 1
        hTp = carve(8 * 516, BF16).rearrange("p (k t) -> p k t", k=8)
        hTs = carve(8 * (nS * 128 + 2), BF16).rearrange("p (k t) -> p k t", k=8)
        NJ = 3
        WU = [carve(8 * 2 * NJ * 128, BF16).rearrange("p (k n) -> p k n", k=8) for _ in range(2)]
        WD = [carve(NJ * D, BF16).rearrange("p (j n) -> p j n", j=NJ) for _ in range(2)]
        WDs = [carve(NJ * D, BF16).rearrange("p (j n) -> p j n", j=NJ) for _ in range(2)]
        aT_rot = Rot([(carve(NJ * 384, BF16).rearrange("p (j t) -> p j t", j=NJ), f"aT{i}") for i in range(3)])
        xn_rot = Rot([(carve(D, BF16), f"xn{i}") for i in range(1)])
        st_rot = Rot([(carve(8, F32), f"st{i}") for i in range(4)])
        cv_rot = Rot([(carve(384, F32), f"cv{i}") for i in range(6)])
        up_banks = Rot([(pq[0], "pq0"), (pq[1], "pq1"), (Sps[0], "S0"), (Sps[1], "S1")])
        dn_banks = Rot([(yps[0], "y0"), (yps[1], "y1"), (Ops, "Ops")])
        P.add("pool", lambda e: e.memset(hTp[:, :, :], 0.0), w=["hTp"])
        P.add("pool", lambda e: e.memset(hTs[:, :, 0:1], 0.0), w=["hTs_e"])
        P.add("pool", lambda e: e.memset(hTs[:, :, nS * 128 + 1:nS * 128 + 2], 0.0), w=["hTs_e"])
        for stream, blist in ((0, [0, 1, 2, 3]), (1, list(range(s_first, s_last + 1)))):
            for b in blist:
                xap, xkey = xres(stream, b)
                xn, xnk = norm_block(xap, xkey)
                for kc in range(8):
                    tr(tpb[:, kc * 128:(kc + 1) * 128], xn[:, kc * 128:(kc + 1) * 128], id_b[:], [xnk, "id_b"], ["tpb"])
                if stream == 0:
                    c0 = (b // 2) * 258 + 1 + (b % 2) * 128; hk = "hTp"
                    dst3 = hTp
                else:
                    c0 = 1 + (b - s_first) * 128; hk = f"hTs{b}"
                    dst3 = hTs
                for kc in range(8):
                    dst = dst3[:, kc, c0:c0 + 128]
                    a_ = AB[:, stream, 2, kc:kc + 1]; b_ = AB[:, stream, 3, kc:kc + 1]
                    if kc % 2 == 0:
                        act(dst, tpb[:, kc * 128:(kc + 1) * 128], AF.Identity, ["tpb", f"AB{L}1"], [hk], bias=b_, scale=a_)
                    else:
                        ts("dve", dst, tpb[:, kc * 128:(kc + 1) * 128], a_, b_, ALU.mult, ALU.add, ["tpb", f"AB{L}1"], [hk])
        for fi, u in enumerate((511, 1536)):
            bu = u // 128
            if s_first <= bu <= s_last:
                c0 = 1 + u - s_first * 128
                ts("dve", hTs[:, :, c0], hTs[:, :, c0], flg[:, fi:fi + 1], None, ALU.mult, None, [f"hTs{bu}", "flg"], [f"hTs{bu}"])
        tiles = [(0, [0, 1], hTp, 0, ["hTp"]), (0, [2, 3], hTp, 258, ["hTp"])]
        fb = s_first if isA else 4
        lb = 13 if isA else 11
        b = fb
        while b <= lb:
            tb = list(range(b, min(b + 3, lb + 1)))
            keys = [f"hTs{x}" for x in range(max(tb[0] - 1, s_first), min(tb[-1] + 1, s_last) + 1)] + ["hTs_e"]
            tiles.append((1, tb, hTs, (tb[0] - s_first) * 128, keys))
            b += 3
        wupv = wup[L].rearrange("(kc p) n -> p kc n", p=128)
        wdnv = wdn[L].rearrange("(j p) n -> p j n", p=128)
        groups = [(0, 3), (3, 3), (6, 3), (9, 3), (12, 3), (15, 3), (18, 2), (20, 2)]
        for gi, (j0, nj) in enumerate(groups):
            if FFN_SUB < 2:
                break
            wb = gi % 2
            P.dma(WU[wb][:, :, 0:nj * 128], wupv[:, :, j0 * 128:(j0 + nj) * 128], w=[f"WU{wb}"], eng="pool")
            P.dma(WU[wb][:, :, NJ * 128:NJ * 128 + nj * 128], wupv[:, :, DFF + j0 * 128:DFF + (j0 + nj) * 128], w=[f"WU{wb}"], eng="pool")
            P.dma(WD[wb][:, 0:nj, :], wdnv[:, j0:j0 + nj, :], w=[f"WD{wb}"], eng="pool")
            cur_c = None
            for (stream, tb, hsrc, cb, hkeys) in tiles:
                if stream != cur_c:
                    cur_c = stream
                    tt("pool", WDs[wb][:, 0:nj, :], WD[wb][:, 0:nj, :],
                       Gb[:, stream, :].unsqueeze(1).broadcast_to([128, nj, D]), ALU.mult, [f"WD{wb}", f"Gb{stream}"], [f"WDs{wb}"])
                if FFN_SUB < 3:
                    continue
                n = len(tb) * 128
                n1 = n + 2
                aT, aTk = aT_rot.get()
                for jj in range(nj):
                    cvs = []
                    for isval in range(2):
                        ch = j0 + jj + 22 * isval
                        wc = isval * NJ * 128 + jj * 128
                        bank, bk = up_banks.get()
                        for kc in range(8):
                            mm(bank[:, 0:n1], WU[wb][:, kc, wc:wc + 128], hsrc[:, kc, cb:cb + n1], kc == 0, kc == 7, [f"WU{wb}"] + hkeys, [bk])
                        cv, cvk = cv_rot.get()
                        w0 = cwT[:, L, ch:ch + 1]; w1 = cwT[:, L, 44 + ch:44 + ch + 1]; w2 = cwT[:, L, 88 + ch:88 + ch + 1]
                        act(cv[:, 0:n], bank[:, 1:n + 1], AF.Identity, [bk, "cwT", "cbT"], [cvk], bias=cbT[:, L, ch:ch + 1], scale=w1)
                        stt("dve", cv[:, 0:n], bank[:, 0:n], w0, cv[:, 0:n], ALU.mult, ALU.add, [bk, cvk, "cwT"], [cvk])
                        stt("dve", cv[:, 0:n], bank[:, 2:n + 2], w2, cv[:, 0:n], ALU.mult, ALU.add, [bk, cvk, "cwT"], [cvk])
                        cvs.append((cv, cvk))
                    if FFN_SUB < 4:
                        continue
                    (gcv, gk), (vcv, vk) = cvs
                    act(gcv[:, 0:n], gcv[:, 0:n], AF.Silu, [gk], [gk])
                    tt("pool", aT[:, jj, 0:n], gcv[:, 0:n], vcv[:, 0:n], ALU.mult, [gk, vk], [aTk])
                for bi, b in enumerate(tb):
                    if FFN_SUB < 5:
                        continue
                    xap, xkey = xres(stream, b)
                    for half in range(2):
                        ybank, yk = dn_banks.get()
                        for jj in range(nj):
                            mm(ybank[:, :], aT[:, jj, bi * 128:(bi + 1) * 128], WDs[wb][:, jj, half * 512:(half + 1) * 512],
                               jj == 0, jj == nj - 1, [aTk, f"WDs{wb}"], [yk])
                        tt("dve", xap[:, half * 512:(half + 1) * 512], ybank[:, :], xap[:, half * 512:(half + 1) * 512], ALU.add,
                           [yk, xkey], [xkey])

    try:
        adaln(0)
        for L in range(2):
            layer(L)
            if L == 0:
                adaln(1)
            chk(7 + 10 * L)
    except StopBuild:
        pass
    P.mark("out")
    for pb in range(4):
        P.dma(yp[pb * 128:(pb + 1) * 128, :], XRp[:, pb, :], r=[f"XRp{pb}"])
    for i, sbk in enumerate(range(4, 12)):
        P.dma(ys[i * 128:(i + 1) * 128, :], XRs[:, sbk - 1, :], r=[f"XRs{sbk}"])
    P.emit()
    return nc, P.stats


def _consts():
    ident = np.eye(128, dtype=np.float32)
    kl = np.arange(128)[:, None]; ql = np.arange(128)[None, :]
    mprev = (kl >= ql).astype(np.float32)
    mnext = (kl <= ql).astype(np.float32)
    col = np.arange(64)
    cs = np.clip(col - 8, 0, 48)
    col_ok = (col[None, :] >= cs[:, None]) & (col[None, :] < cs[:, None] + 16)
    cm = np.where(col_ok, 0.0, NEG).astype(np.float32)
    cmask = np.tile(cm, (2, 2))
    negr = np.zeros((2, 128), np.float32)
    negr[0, 0:64] = NEG / SCALE
    negr[1, 64:128] = NEG / SCALE
    return ident, mprev, mnext, cmask, negr


def _rope_tables(t):
    half = 16
    freqs = (10000.0 ** (-np.arange(half, dtype=np.float32) / np.float32(half))).astype(np.float32)
    row = (t // 64).astype(np.float32); colp = (t % 64).astype(np.float32)
    ar = row[:, None] * freqs[None, :]; ac = colp[:, None] * freqs[None, :]
    cr, sr, cc, sc = np.cos(ar), np.sin(ar), np.cos(ac), np.sin(ac)
    C = np.concatenate([cr, cr, cc, cc], axis=1).astype(np.float32)
    S = np.concatenate([-sr, sr, -sc, sc], axis=1).astype(np.float32)
    return C, S


def make_in_maps(x_prompt, x_sample, cache_k_a, cache_v_a, cache_k_b, cache_v_b, c, c_ctx,
                 norm_attn_w, norm_ffn_w, w_ada, b_ada, w_qkv_a, q_norm_a, k_norm_a, sink_a, w_o_a,
                 w_qkv_b, q_norm_b, k_norm_b, rpb_b, w_o_b, w_up, conv_w, conv_b, w_down):
    f = np.float32
    x_prompt = np.asarray(x_prompt, f); x_sample = np.asarray(x_sample, f)
    ident, mprev, mnext, cmask, negr = _consts()
    wqa = np.asarray(w_qkv_a, f)[0]; wqb = np.asarray(w_qkv_b, f)[0]
    wga = np.stack([np.concatenate([wqa[:, g * 256:(g + 1) * 256], wqa[:, 1024 + g * 64:1024 + (g + 1) * 64],
                                    wqa[:, 1024 + g * 64:1024 + (g + 1) * 64], wqa[:, 1280 + g * 64:1280 + (g + 1) * 64]], axis=1)
                    for g in range(4)])
    wgb = np.stack([np.concatenate([wqb[:, g * 256:(g + 1) * 256], wqb[:, 1024 + g * 256:1024 + (g + 1) * 256],
                                    wqb[:, 2048 + g * 256:2048 + (g + 1) * 256]], axis=1) for g in range(4)])
    qa = np.asarray(q_norm_a, f)[0]; ka = np.asarray(k_norm_a, f)[0]
    qb_ = np.asarray(q_norm_b, f)[0]; kb_ = np.asarray(k_norm_b, f)[0]
    qkwa = np.concatenate([np.tile(qa, 4), np.tile(ka, 2)])[None, :]
    qkwb = np.concatenate([np.tile(qb_, 4), np.tile(kb_, 4)])[None, :]
    nw = np.stack([np.asarray(norm_attn_w, f), np.asarray(norm_ffn_w, f)], axis=1).reshape(32, 128)
    bada = np.asarray(b_ada, f).reshape(2, 48, 128)
    convw = np.asarray(conv_w, f).reshape(2, 3, 44, 128).reshape(2, 132, 128)
    convb = np.asarray(conv_b, f).reshape(2, 44, 128)
    rpb = np.asarray(rpb_b, f)[0]
    bq = np.arange(128) // 64; qc = np.arange(128) % 64
    ak = np.arange(128) // 64; kc = np.arange(128) % 64
    dc = np.clip(kc[None, :] - qc[:, None], -15, 15) + 15
    btab = np.zeros((4, 128, 4, 7, 128), f)
    for e in range(7):
        dr = np.clip(2 * (e - 3) + ak[None, :] - bq[:, None], -7, 7) + 7
        for g in range(4):
            for hh in range(4):
                btab[g, :, hh, e, :] = rpb[g * 4 + hh][dr, dc]
    btab = btab.reshape(4, 128, 4 * 7 * 128)
    shared = dict(wada=np.asarray(w_ada, f), wga=wga, wgb=wgb, woa=np.asarray(w_o_a, f)[0], wob=np.asarray(w_o_b, f)[0],
                  wup=np.asarray(w_up, f), wdn=np.asarray(w_down, f), qkwa=qkwa, qkwb=qkwb, sink=np.asarray(sink_a, f),
                  nw=nw, bada=bada, convw=convw, convb=convb, ident=ident, mprev=mprev, mnext=mnext,
                  btab=btab, cmask=cmask, negr=negr)
    in_maps = []
    for core in range(8):
        bs = core // 4; qt = core % 4; t0 = qt * 1024; R0 = qt * 16
        t = np.arange(NSLAB * 128) + t0 - 512
        ok = (t >= 0) & (t < DEC_SEQ)
        xs = np.zeros((NSLAB * 128, D), f)
        xs[ok] = x_sample[bs, t[ok]]
        C, S = _rope_tables(t)
        vala = ok.astype(f).reshape(NSLAB, 128).T.copy()
        flags = np.ones((128, 2), f)
        if qt == 0:
            flags[:, 0] = 0.0
        if qt == 3:
            flags[:, 1] = 0.0
        inv = np.zeros((2, NUNIT, 2, 64), f)
        for (qb, kb), u in B_UNITS.items():
            for a in range(2):
                kr = R0 - 8 + 2 * kb + a
                for b in range(2):
                    r = R0 - 8 + 2 * qb + b
                    rs = min(max(r - 4, 0), 56)
                    valid = (0 <= kr < 64) and (rs <= kr < rs + 8)
                    inv[a, u, b, :] = 0.0 if valid else 1.0
        cond = np.stack([np.asarray(c_ctx, f).reshape(8, 128), np.asarray(c, f)[bs].reshape(8, 128)], axis=1).reshape(16, 128)
        m = dict(shared)
        m.update(xp=x_prompt[2 * core:2 * core + 2].reshape(512, D), xs=xs, cond=cond,
                 cka=np.asarray(cache_k_a, f)[bs, 0].reshape(PAST, 256), cva=np.asarray(cache_v_a, f)[bs, 0].reshape(PAST, 256),
                 ckb=np.asarray(cache_k_b, f)[bs, 0].reshape(PAST, D), cvb=np.asarray(cache_v_b, f)[bs, 0].reshape(PAST, D),
                 ropeC=C, ropeS=S, vala=vala, flags=flags, invt=inv.reshape(2, NUNIT * 128))
        in_maps.append({k: np.ascontiguousarray(v, dtype=f) for k, v in m.items()})
    return in_maps


def kernel(**inputs):
    f = np.float32
    in_maps = make_in_maps(**inputs)
    nc, stats = build()
    res = run_bass_kernel_spmd(nc, in_maps, core_ids=list(range(8)))
    R = res.results
    y_p = np.stack([R[cq]["yp"].reshape(2, SEQ, D) for cq in range(8)]).reshape(16, SEQ, D)
    y_s = np.stack([np.concatenate([R[b * 4 + q]["ys"] for q in range(4)], axis=0) for b in range(2)])
    nka = np.concatenate([R[cq]["nka"].reshape(2, 1, SEQ, 4, HD) for cq in range(8)], axis=0)
    nva = np.concatenate([R[cq]["nva"].reshape(2, 1, SEQ, 4, HD) for cq in range(8)], axis=0)
    nkb = np.concatenate([R[cq]["nkb"].reshape(2, 1, SEQ, 16, HD) for cq in range(8)], axis=0)
    nvb = np.concatenate([R[cq]["nvb"].reshape(2, 1, SEQ, 16, HD) for cq in range(8)], axis=0)
    return (y_p.astype(f), y_s.astype(f), nka.astype(f), nva.astype(f), nkb.astype(f), nvb.astype(f))
```
